# Optimizing a Trainium2 kernel written in Bass

```python
import jax, jax.numpy as jnp
from jax import lax
import numpy as np

D_MODEL = 1024
BATCH = 4
SEQ = 4096
DEPTH = 2

D_MIX = 1024
RWKV_HEAD_DIM = 64
RWKV_HEADS = 6
RWKV_DIM = 384
W_LORA = 32
A_LORA = 32
G_LORA = 64
RWKV_GN_EPS = 64e-5
CONV_DIM = 256
CONV_WIDTH = 31
LN_EPS = 1e-5
MLA_HEADS = 6
QK_NOPE_DIM = 64
QK_ROPE_DIM = 32
V_HEAD_DIM = 64
MLA_DIM = 384
Q_LORA = 256
KV_LORA = 256
ROPE_THETA = 10000.0
Q_BLOCK = 128
D_FF = 4 * D_MODEL
NORM_EPS = 1e-6

P_RWKV = 3 * RWKV_DIM + W_LORA + A_LORA + G_LORA
P_CONV = 2 * CONV_DIM
P_MLA = Q_LORA + KV_LORA + QK_ROPE_DIM
P_IN = P_RWKV + P_CONV + P_MLA

kernel_name = "hybrid_rwkv7_conformer_mla_block"


def rms_norm(x, g):
    xf = x.astype(jnp.float32)
    y = xf * lax.rsqrt(jnp.mean(xf * xf, axis=-1, keepdims=True) + NORM_EPS)
    return (y * g.astype(jnp.float32)).astype(x.dtype)


def layer_norm(x, w, b, eps):
    xf = x.astype(jnp.float32)
    mu = jnp.mean(xf, axis=-1, keepdims=True)
    var = jnp.mean(jnp.square(xf - mu), axis=-1, keepdims=True)
    y = (xf - mu) * lax.rsqrt(var + eps)
    return (y * w.astype(jnp.float32) + b.astype(jnp.float32)).astype(x.dtype)


def rwkv7_step(S, inp):
    r_t, w_t, k_t, v_t, a_t, b_t = inp
    Sa = jnp.einsum('bhvk,bhk->bhv', S, a_t)
    S = S * w_t[:, :, None, :] + Sa[..., None] * b_t[:, :, None, :] + v_t[..., None] * k_t[:, :, None, :]
    y = jnp.einsum('bhvk,bhk->bhv', S, r_t)
    return S, y


def rwkv7_time_mix(p, mu, w0, w2, a0, a2, g2, k_k, k_a, r_k, ln_w, ln_b):
    B, T, _ = p.shape
    H, N = RWKV_HEADS, RWKV_HEAD_DIM
    prev = jnp.pad(p, ((0, 0), (1, 0), (0, 0)))[:, :T]
    z = p + (prev - p) * mu
    cuts = [RWKV_DIM, 2 * RWKV_DIM, 3 * RWKV_DIM, 3 * RWKV_DIM + W_LORA, 3 * RWKV_DIM + W_LORA + A_LORA]
    r, k, v, w_lo, a_lo, g_lo = jnp.split(z, cuts, axis=-1)
    w = -jax.nn.softplus(-(w0 + jnp.tanh(w_lo) @ w2)) - 0.5
    a = jax.nn.sigmoid(a0 + a_lo @ a2)
    g = jax.nn.sigmoid(g_lo) @ g2
    kk = (k * k_k).reshape(B, T, H, N).astype(jnp.float32)
    kk = kk * lax.rsqrt(jnp.maximum(jnp.sum(kk * kk, axis=-1, keepdims=True), 1e-24))
    k = k * (1 + (a - 1) * k_a)

    def heads(t):
        return t.reshape(B, T, H, N).astype(jnp.float32)

    r_h, k_h, v_h, a_h = heads(r), heads(k), heads(v), heads(a)
    decay = jnp.exp(-jnp.exp(heads(w)))

    def tmaj(t):
        return jnp.moveaxis(t, 1, 0)

    xs = (tmaj(r_h), tmaj(decay), tmaj(k_h), tmaj(v_h), tmaj(-kk), tmaj(kk * a_h))
    S0 = jnp.zeros((B, H, N, N), jnp.float32)
    _, y = lax.scan(rwkv7_step, S0, xs)
    y = jnp.moveaxis(y, 0, 1)
    m = jnp.mean(y, axis=-1, keepdims=True)
    var = jnp.mean(jnp.square(y - m), axis=-1, keepdims=True)
    y = ((y - m) * lax.rsqrt(var + RWKV_GN_EPS)).reshape(B, T, RWKV_DIM)
    y = y * ln_w.astype(jnp.float32) + ln_b.astype(jnp.float32)
    bonus = jnp.sum(r_h * k_h * r_k.astype(jnp.float32), axis=-1, keepdims=True) * v_h
    y = y + bonus.reshape(B, T, RWKV_DIM)
    return y.astype(p.dtype) * g


def conformer_conv(p, conv_w, conv_b, ln_w, ln_b):
    u, gt = jnp.split(p, 2, axis=-1)
    h = u * jax.nn.sigmoid(gt)
    h = lax.conv_general_dilated(
        h, conv_w[:, None, :], window_strides=(1,), padding=((CONV_WIDTH - 1, 0),),
        dimension_numbers=('NWC', 'WIO', 'NWC'), feature_group_count=CONV_DIM) + conv_b
    h = layer_norm(h, ln_w, ln_b, LN_EPS)
    return jax.nn.silu(h)


def rope_cos_sin(positions):
    inv_freq = 1.0 / (ROPE_THETA ** (jnp.arange(0, QK_ROPE_DIM, 2, dtype=jnp.float32) / QK_ROPE_DIM))
    ang = positions.astype(jnp.float32)[..., None] * inv_freq
    ang = jnp.concatenate([ang, ang], axis=-1)
    return jnp.cos(ang), jnp.sin(ang)


def apply_rope(x, cos, sin):
    xf = x.astype(jnp.float32)
    x1, x2 = jnp.split(xf, 2, axis=-1)
    rot = jnp.concatenate([-x2, x1], axis=-1)
    return (xf * cos + rot * sin).astype(x.dtype)


def mla_attention(p, positions, q_norm, w_uq, kv_norm, w_ukv):
    B, T, _ = p.shape
    H = MLA_HEADS
    q_c, c_kv, k_pe = jnp.split(p, [Q_LORA, Q_LORA + KV_LORA], axis=-1)
    q = (rms_norm(q_c, q_norm) @ w_uq).reshape(B, T, H, QK_NOPE_DIM + QK_ROPE_DIM)
    q_nope, q_pe = q[..., :QK_NOPE_DIM], q[..., QK_NOPE_DIM:]
    kv = (rms_norm(c_kv, kv_norm) @ w_ukv).reshape(B, T, H, QK_NOPE_DIM + V_HEAD_DIM)
    k_nope, v = kv[..., :QK_NOPE_DIM], kv[..., QK_NOPE_DIM:]
    cos, sin = rope_cos_sin(positions)
    q_pe = apply_rope(q_pe, cos[:, :, None, :], sin[:, :, None, :])
    k_pe = apply_rope(k_pe, cos, sin)
    q = jnp.concatenate([q_nope, q_pe], axis=-1).transpose(0, 2, 1, 3)
    k = jnp.concatenate([k_nope, jnp.broadcast_to(k_pe[:, :, None, :], (B, T, H, QK_ROPE_DIM))],
                        axis=-1).transpose(0, 2, 1, 3)
    v = v.transpose(0, 2, 1, 3)
    scale = (QK_NOPE_DIM + QK_ROPE_DIM) ** -0.5
    outs = []
    for i in range(T // Q_BLOCK):
        qs, ke = i * Q_BLOCK, (i + 1) * Q_BLOCK
        s = jnp.einsum('bhqd,bhkd->bhqk', q[:, :, qs:ke], k[:, :, :ke]).astype(jnp.float32) * scale
        mask = (qs + jnp.arange(Q_BLOCK))[:, None] >= jnp.arange(ke)[None, :]
        s = jnp.where(mask, s, -jnp.inf)
        pr = jax.nn.softmax(s, axis=-1).astype(v.dtype)
        outs.append(jnp.einsum('bhqk,bhkd->bhqd', pr, v[:, :, :ke]))
    o = jnp.concatenate(outs, axis=2)
    return o.transpose(0, 2, 1, 3).reshape(B, T, MLA_DIM)


def setup_inputs(seed: int = 0) -> dict:
    key = jax.random.key(seed)
    keys = iter(jax.random.split(key, 40))
    L, D = DEPTH, D_MODEL
    f32 = jnp.float32

    def normal(shape, scale):
        return scale * jax.random.normal(next(keys), shape, f32)

    def gain(shape):
        return 1.0 + normal(shape, 0.05)

    def uniform(shape, lo, hi):
        return jax.random.uniform(next(keys), shape, f32, lo, hi)

    x = normal((BATCH, SEQ, D), 1.0)
    c = normal((BATCH, D), 1.0)
    offset = jax.random.randint(next(keys), (BATCH, 1), 0, 1024, dtype=jnp.int32)
    positions = offset + jnp.arange(SEQ, dtype=jnp.int32)[None, :]
    return {
        "x": x,
        "c": c,
        "positions": positions,
        "g_pre_mix": gain((L, D)),
        "g_post_mix": gain((L, D)),
        "g_pre_ffn": gain((L, D)),
        "g_post_ffn": gain((L, D)),
        "w_ada": normal((L, D, 6 * D), 0.5 * D ** -0.5),
        "b_ada": normal((L, 6 * D), 0.02),
        "w_in": normal((L, D, P_IN), D ** -0.5),
        "w_out": normal((L, D_MIX, D), D_MIX ** -0.5),
        "rwkv_mu": uniform((L, P_RWKV), 0.0, 1.0),
        "rwkv_w0": uniform((L, RWKV_DIM), -6.0, -1.0),
        "rwkv_w2": normal((L, W_LORA, RWKV_DIM), W_LORA ** -0.5),
        "rwkv_a0": normal((L, RWKV_DIM), 0.5),
        "rwkv_a2": normal((L, A_LORA, RWKV_DIM), A_LORA ** -0.5),
        "rwkv_g2": normal((L, G_LORA, RWKV_DIM), G_LORA ** -0.5),
        "rwkv_k_k": 0.85 + normal((L, RWKV_DIM), 0.05),
        "rwkv_k_a": gain((L, RWKV_DIM)),
        "rwkv_r_k": normal((L, RWKV_HEADS, RWKV_HEAD_DIM), 0.1),
        "rwkv_ln_w": gain((L, RWKV_DIM)),
        "rwkv_ln_b": normal((L, RWKV_DIM), 0.02),
        "conv_w": normal((L, CONV_WIDTH, CONV_DIM), CONV_WIDTH ** -0.5),
        "conv_b": normal((L, CONV_DIM), 0.02),
        "conv_ln_w": gain((L, CONV_DIM)),
        "conv_ln_b": normal((L, CONV_DIM), 0.02),
        "mla_q_norm": gain((L, Q_LORA)),
        "mla_w_uq": normal((L, Q_LORA, MLA_HEADS * (QK_NOPE_DIM + QK_ROPE_DIM)), Q_LORA ** -0.5),
        "mla_kv_norm": gain((L, KV_LORA)),
        "mla_w_ukv": normal((L, KV_LORA, MLA_HEADS * (QK_NOPE_DIM + V_HEAD_DIM)), KV_LORA ** -0.5),
        "w_ff1": normal((L, D, D_FF), D ** -0.5),
        "w_ff2": normal((L, D_FF, D), D_FF ** -0.5),
    }


def reference(x, c, positions, g_pre_mix, g_post_mix, g_pre_ffn, g_post_ffn, w_ada, b_ada,
              w_in, w_out, rwkv_mu, rwkv_w0, rwkv_w2, rwkv_a0, rwkv_a2, rwkv_g2, rwkv_k_k,
              rwkv_k_a, rwkv_r_k, rwkv_ln_w, rwkv_ln_b, conv_w, conv_b, conv_ln_w, conv_ln_b,
              mla_q_norm, mla_w_uq, mla_kv_norm, mla_w_ukv, w_ff1, w_ff2):
    cs = jax.nn.silu(c)
    for l in range(DEPTH):
        mod = (cs @ w_ada[l] + b_ada[l])[:, None, :]
        sh1, sc1, gt1, sh2, sc2, gt2 = jnp.split(mod, 6, axis=-1)

        h = rms_norm(x, g_pre_mix[l]) * (1 + sc1) + sh1
        p = h @ w_in[l]
        p_rwkv, p_conv, p_mla = jnp.split(p, [P_RWKV, P_RWKV + P_CONV], axis=-1)
        y_a = rwkv7_time_mix(p_rwkv, rwkv_mu[l], rwkv_w0[l], rwkv_w2[l], rwkv_a0[l], rwkv_a2[l],
                             rwkv_g2[l], rwkv_k_k[l], rwkv_k_a[l], rwkv_r_k[l], rwkv_ln_w[l], rwkv_ln_b[l])
        y_b = conformer_conv(p_conv, conv_w[l], conv_b[l], conv_ln_w[l], conv_ln_b[l])
        y_c = mla_attention(p_mla, positions, mla_q_norm[l], mla_w_uq[l], mla_kv_norm[l], mla_w_ukv[l])
        y = jnp.concatenate([y_a, y_b, y_c], axis=-1) @ w_out[l]
        x = x + gt1 * rms_norm(y, g_post_mix[l])

        h = rms_norm(x, g_pre_ffn[l]) * (1 + sc2) + sh2
        y = jnp.square(jax.nn.relu(h @ w_ff1[l])) @ w_ff2[l]
        x = x + gt2 * rms_norm(y, g_post_ffn[l])
    return x
```

```python
import os
import numpy as np
from contextlib import ExitStack
import concourse.bass as bass
import concourse.mybir as mybir
from concourse.bass_utils import run_bass_kernel_spmd

F32 = mybir.dt.float32
BF16 = mybir.dt.bfloat16
I32 = mybir.dt.int32
AF = mybir.ActivationFunctionType
ALU = mybir.AluOpType

D = 1024
NT = 512
L = 2
C0 = float(np.exp(-0.5))
GPM, GQM, GPF, GQF, BADA, MU, W0, A0, KK, KA, RK, LNW, LNB, CB, CLW, CLB, QN, KVN, CW = (
    0, 8, 16, 24, 32, 80, 90, 93, 96, 99, 102, 105, 108, 111, 113, 115, 117, 119, 121)
NCOL = 121 + 62
OMU, OMKA, MOD, GM1, GG1, GM2, GG2, NDC = 0, 10, 13, 61, 69, 77, 85, 93
K_ID, K_ONES, K_BLK, K_SU, K_SL, K_UIS, K_CM, K_RM, K_SWAP, K_ROT, K_IF, NCONST = (
    0, 128, 256, 384, 512, 640, 704, 832, 1344, 1472, 1600, 1601)


class Region:
    __slots__ = ("name", "w", "r", "dsem", "dcnt")

    def __init__(self, name):
        self.name = name
        self.w = None
        self.r = {}
        self.dsem = None
        self.dcnt = 0


class Eng:
    def __init__(self, name, obj, sem):
        self.name = name
        self.obj = obj
        self.sem = sem
        self.cnt = 0
        self.seen = {}


class KB:
    def __init__(self, nc, es):
        self.nc = nc
        self.es = es
        self.nsem = 0
        self.E = {}
        for n, o in (("pe", nc.tensor), ("act", nc.scalar), ("dve", nc.vector), ("pool", nc.gpsimd), ("sp", nc.sync)):
            self.E[n] = Eng(n, o, self.newsem("e_" + n))
        self.nins = 0
        self.dsems = {}

    def newsem(self, name):
        self.nsem += 1
        return self.es.enter_context(self.nc.semaphore(name + "_%d" % self.nsem))

    def _waits(self, e, reads, writes, skip_dma_waw=None):
        need = {}

        def add(ev):
            if ev is None:
                return
            sem, val = ev
            if sem is e.sem:
                return
            if e.seen.get(id(sem), 0) >= val:
                return
            if need.get(id(sem), (None, 0))[1] < val:
                need[id(sem)] = (sem, val)

        for R in reads:
            add(R.w)
        for R in writes:
            if not (skip_dma_waw is not None and R.w is not None and R.w[0] is skip_dma_waw):
                add(R.w)
            for ev in R.r.values():
                add(ev)
        for sem, val in need.values():
            e.obj.wait_ge(sem, val)
            e.seen[id(sem)] = val

    def op(self, en, fn, reads=(), writes=()):
        e = self.E[en]
        self._waits(e, reads, writes)
        ins = fn(e.obj)
        e.cnt += 1
        ins.then_inc(e.sem, 1)
        ev = (e.sem, e.cnt)
        for R in reads:
            R.r[id(e.sem)] = ev
        for R in writes:
            R.w = ev
            R.r = {}
        self.nins += 1
        return ins

    def dma(self, qn, out, in_, reads=(), writes=(), accumulate=False):
        q = self.E[qn]
        W = writes[0]
        if W.name not in self.dsems:
            self.dsems[W.name] = [self.newsem("d_" + W.name), 0]
        ds = self.dsems[W.name]
        W.dsem = ds[0]
        self._waits(q, reads, writes, skip_dma_waw=W.dsem if accumulate else None)
        ins = q.obj.dma_start(out=out, in_=in_)
        ds[1] += 16
        ins.then_inc(W.dsem, 16)
        ev = (W.dsem, ds[1])
        for R in reads:
            R.r[id(W.dsem)] = ev
        for R in writes:
            R.w = ev
            R.r = {}
        self.nins += 1

    def barrier(self, regions=()):
        ce = ["pe", "act", "dve", "pool"]
        for a in ce:
            ea = self.E[a]
            for b in ce:
                if a == b:
                    continue
                eb = self.E[b]
                if eb.cnt > 0 and ea.seen.get(id(eb.sem), 0) < eb.cnt:
                    ea.obj.wait_ge(eb.sem, eb.cnt)
                    ea.seen[id(eb.sem)] = eb.cnt
            for R in regions:
                evs = list(R.r.values()) + ([R.w] if R.w else [])
                for sem, val in evs:
                    if sem is ea.sem or ea.seen.get(id(sem), 0) >= val:
                        continue
                    ea.obj.wait_ge(sem, val)
                    ea.seen[id(sem)] = val

    def finish(self, regions):
        e = self.E["sp"]
        for R in regions:
            if R.w is not None:
                sem, val = R.w
                if e.seen.get(id(sem), 0) < val:
                    e.obj.wait_ge(sem, val)
                    e.seen[id(sem)] = val


class _Stop(Exception):
    pass


def build(T, dbg=None):
    NTL = T // NT
    NB = T // 128
    nc = bass.Bass("TRN2", target_bir_lowering=False)

    def din(name, shape, dt=F32):
        return nc.dram_tensor(name, list(shape), dt, kind="ExternalInput").ap()

    xT_d = din("xT", [D, T])
    cT_d = din("cT", [128, 8])
    pos_d = din("pos", [1, T], I32)
    wada_d = din("wada", [L, D, 6 * D])
    win_d = din("win", [L, D, 2400])
    wout_d = din("wout", [L, D, D])
    ff1_d = din("ff1", [L, D, 4 * D])
    ff2_d = din("ff2", [L, 4 * D, D])
    wl_d = din("wl", [L, 128, 1152])
    wuq_d = din("wuq", [L, 256, 576])
    wk_d = din("wk", [L, 256, 384])
    wv_d = din("wv", [L, 256, 384])
    cols_d = din("cols", [L, 128, NCOL])
    const_d = din("consts", [128, NCONST])
    oT_d = nc.dram_tensor("oT", [D, T], F32, kind="ExternalOutput").ap()
    kc_d = nc.dram_tensor("kcache", [L, 6, 96, T], BF16, kind="Internal").ap()
    vc_d = nc.dram_tensor("vcache", [L, 6, 128, NB, 64], BF16, kind="Internal").ap()
    dbg_d = None
    if dbg is not None:
        dbg_d = nc.dram_tensor("dbg", [128, 16 * 512], F32, kind="ExternalOutput").ap()
    dbg_slots = {}
    R_dbg = Region("dbg")

    es = ExitStack()
    with es:
        k = KB(nc, es)

        def sb(name, shape, dt=F32):
            return es.enter_context(nc.sbuf_tensor(name, list(shape), dt))

        def ps(name):
            return es.enter_context(nc.psum_tensor(name, [128, 512], F32))

        xT = sb("xT_s", [128, 8, NT]); R_x = [Region("x%d" % i) for i in range(8)]
        hT = sb("hT_s", [128, 8, NT], BF16); R_h = Region("hT")
        mixT = sb("mixT_s", [128, 8, NT], BF16); R_mix = [Region("mix%d" % i) for i in range(8)]
        big = sb("big_s", [128, 8192]); R_big = Region("big")
        pT = big[:, 0:5120].rearrange("p (c t) -> p c t", t=NT)
        hid = big[:].bitcast(BF16).rearrange("p (c t) -> p c t", t=NT)
        wbuf = [sb("wbuf%d" % i, [128, 8192], BF16) for i in range(2)]
        R_wb = [Region("wb%d" % i) for i in range(2)]
        wl = sb("wl_s", [128, L, 1152], BF16); wuq = sb("wuq_s", [128, L, 2, 576], BF16)
        wk = sb("wk_s", [128, L, 2, 384], BF16); wv = sb("wv_s", [128, L, 2, 384], BF16)
        R_sw = Region("smallw")
        cols = sb("cols_s", [128, L, NCOL]); R_cols = Region("cols")
        dcol = sb("dcol_s", [128, L, NDC]); R_dcol = Region("dcol")
        cst = sb("cst_s", [128, NCONST]); R_cst = Region("cst")
        cstb = sb("cstb_s", [128, 256], BF16)
        cs_c = sb("csc_s", [128, 8]); cs_b = sb("csb_s", [128, 8], BF16); R_cs = Region("cs")
        hbuf = [sb("hbuf%d" % l, [128, 2, 30 + NT]) for l in range(L)]
        R_hb = [Region("hb%d" % l) for l in range(L)]
        plast = sb("plast_s", [128, L, 10]); R_pl = [Region("pl%d" % l) for l in range(L)]
        Sbd = [[sb("S%d_%d" % (l, p), [128, 128]) for p in range(3)] for l in range(L)]
        R_S = [[Region("S%d_%d" % (l, p)) for p in range(3)] for l in range(L)]
        kint = sb("kint_s", [128, NT], I32)
        cosT = sb("cos_s", [128, NT]); sinT = sb("sin_s", [128, NT]); R_cos = Region("cossin")
        arena = sb("arena_s", [128, 14336]); R_ar = Region("arena")
        banks = [ps("bank%d" % i) for i in range(8)]
        R_bk = [Region("bank%d" % i) for i in range(8)]
        R_kc = [[Region("kc%d_%d" % (l, h)) for h in range(6)] for l in range(L)]
        R_vc = [[Region("vc%d_%d" % (l, h)) for h in range(6)] for l in range(L)]
        R_out = Region("out")

        ident = cst[:, K_ID:K_ID + 128]
        ones_m = cst[:, K_ONES:K_ONES + 128]
        blk2 = cst[:, K_BLK:K_BLK + 128]
        su2 = cst[:, K_SU:K_SU + 128]
        sl2 = cst[:, K_SL:K_SL + 128]
        uis = cst[:, K_UIS:K_UIS + 64]
        rmask = cst[:, K_RM:K_RM + 512]
        swapm = cst[:, K_SWAP:K_SWAP + 128]
        rotm = cst[:, K_ROT:K_ROT + 128]
        ifr = cst[:, K_IF:K_IF + 1]

        _bk = [0]

        def bank(pool=(2, 3, 4, 5, 6)):
            _bk[0] += 1
            i = pool[_bk[0] % len(pool)]
            return banks[i], R_bk[i]

        def mm(out, lhsT, rhs, start, stop, R, W):
            k.op("pe", lambda e: e.matmul(out, lhsT, rhs, start=start, stop=stop), R, W)

        def act(out, in_, func, R, W, bias=0.0, scale=1.0):
            k.op("act", lambda e: e.activation(out=out, in_=in_, func=func, bias=bias, scale=scale), R, W)

        def tt(en, out, in0, in1, op, R, W):
            k.op(en, lambda e: e.tensor_tensor(out=out, in0=in0, in1=in1, op=op), R, W)

        def ts(en, out, in0, s1, s2, op0, op1, R, W):
            if op1 is None:
                k.op(en, lambda e: e.tensor_scalar(out=out, in0=in0, scalar1=s1, scalar2=None, op0=op0), R, W)
            else:
                k.op(en, lambda e: e.tensor_scalar(out=out, in0=in0, scalar1=s1, scalar2=s2, op0=op0, op1=op1), R, W)

        def stt(en, out, in0, scalar, in1, op0, op1, R, W):
            k.op(en, lambda e: e.scalar_tensor_tensor(out=out, in0=in0, scalar=scalar, in1=in1, op0=op0, op1=op1), R, W)

        def cp(en, out, in_, R, W):
            if en == "act":
                k.op("act", lambda e: e.copy(out=out, in_=in_), R, W)
            else:
                k.op(en, lambda e: e.tensor_copy(out=out, in_=in_), R, W)

        def recip(out, in_, R, W):
            k.op("dve", lambda e: e.reciprocal(out=out, in_=in_), R, W)

        def memset(en, ap, val, W):
            k.op(en, lambda e: e.memset(ap, val), (), W)

        def chk(name, ap, R):
            if dbg is None:
                return
            names, stop = dbg
            if name in names and name not in dbg_slots:
                i = len(dbg_slots)
                dbg_slots[name] = i
                n = ap.shape[-1]
                k.dma("pool", dbg_d[0:ap.shape[0], i * 512:i * 512 + n], ap, R, [R_dbg], accumulate=True)
            if name == stop:
                raise _Stop()

        k.dma("sp", cst[:], const_d, (), [R_cst])
        k.dma("sp", cols[:], cols_d.rearrange("l p n -> p l n"), (), [R_cols])
        k.dma("sp", cs_c[:], cT_d, (), [R_cs])
        R_sw2 = [Region("sw%d" % i) for i in range(4)]
        k.dma("pool", wl[:], wl_d.rearrange("l p n -> p l n"), (), [R_sw2[0]])
        k.dma("pool", wuq[:], wuq_d.rearrange("l (c p) n -> p l c n", p=128), (), [R_sw2[1]])
        k.dma("pool", wk[:], wk_d.rearrange("l (c p) n -> p l c n", p=128), (), [R_sw2[2]])
        k.dma("pool", wv[:], wv_d.rearrange("l (c p) n -> p l c n", p=128), (), [R_sw2[3]])
        R_SW = R_sw2
        cp("dve", cstb[:, 0:128], ident, [R_cst], [R_cst])
        cp("dve", cstb[:, 128:256], cst[:, K_CM:K_CM + 128], [R_cst], [R_cst])
        identb = cstb[:, 0:128]
        cmb = cstb[:, 128:256]
        for l in range(L):
            for p in range(3):
                memset("dve", Sbd[l][p][:], 0.0, [R_S[l][p]])
            memset("dve", hbuf[l][:], 0.0, [R_hb[l]])
            memset("dve", plast[:, l, :], 0.0, [R_pl[l]])
        act(cs_c[:], cs_c[:], AF.Silu, [R_cs], [R_cs])
        cp("dve", cs_b[:], cs_c[:], [R_cs], [R_cs])

        _wb = [0]

        def wload(src_ap, shape_view):
            i = _wb[0] % 2
            _wb[0] += 1
            n = 1
            for s in shape_view[1:]:
                n *= s
            flat = wbuf[i][:, 0:n]
            if len(shape_view) == 3:
                view = flat.rearrange("p (a b) -> p a b", b=shape_view[2])
            else:
                view = flat
            k.dma("pool", view, src_ap, (), [R_wb[i]])
            return view, R_wb[i]

        for l in range(L):
            for g in range(8):
                wv_, Rw = wload(wada_d[l, :, g * 768:(g + 1) * 768].rearrange("(c p) n -> p c n", p=128), [128, 8, 768])
                bk, Rb = bank()
                for j in range(6):
                    for kc in range(8):
                        mm(bk[:, j:j + 1], wv_[:, kc, j * 128:(j + 1) * 128], cs_b[:, kc:kc + 1], kc == 0, kc == 7,
                           [Rw, R_cs], [Rb])
                tt("dve", dcol[:, l, MOD + g * 6:MOD + g * 6 + 6], bk[:, 0:6], cols[:, l, BADA + g * 6:BADA + g * 6 + 6],
                   ALU.add, [Rb, R_cols], [R_dcol])
            m = lambda i: dcol[:, l, MOD + i * 8:MOD + i * 8 + 8]
            stt("dve", dcol[:, l, GM1:GM1 + 8], m(1), 1.0, cols[:, l, GPM:GPM + 8], ALU.add, ALU.mult, [R_dcol, R_cols], [R_dcol])
            tt("dve", dcol[:, l, GG1:GG1 + 8], m(2), cols[:, l, GQM:GQM + 8], ALU.mult, [R_dcol, R_cols], [R_dcol])
            stt("dve", dcol[:, l, GM2:GM2 + 8], m(4), 1.0, cols[:, l, GPF:GPF + 8], ALU.add, ALU.mult, [R_dcol, R_cols], [R_dcol])
            tt("dve", dcol[:, l, GG2:GG2 + 8], m(5), cols[:, l, GQF:GQF + 8], ALU.mult, [R_dcol, R_cols], [R_dcol])
            ts("dve", dcol[:, l, OMU:OMU + 10], cols[:, l, MU:MU + 10], -1.0, 1.0, ALU.mult, ALU.add, [R_cols], [R_dcol])
            ts("dve", dcol[:, l, OMKA:OMKA + 3], cols[:, l, KA:KA + 3], -1.0, 1.0, ALU.mult, ALU.add, [R_cols], [R_dcol])

        def col(l, base, i=0):
            return cols[:, l, base + i:base + i + 1]

        def dc(l, base, i=0):
            return dcol[:, l, base + i:base + i + 1]

        def rms_rstd(out, srcs, Rsrc, nfeat, eps, sq_tmp, R_tmp, lhs=None):
            bk, Rb = bank((0, 1))
            n = len(srcs)
            for i, s in enumerate(srcs):
                st = sq_tmp[i % len(sq_tmp)]
                act(st, s, AF.Square, Rsrc, [R_tmp[i % len(sq_tmp)]])
                mm(bk[:], lhs if lhs is not None else ones_m, st, i == 0, i == n - 1, [R_tmp[i % len(sq_tmp)], R_cst], [Rb])
            act(out[0], bk[:], AF.Sqrt, [Rb], [out[1]], bias=eps, scale=1.0 / nfeat)
            recip(out[0], out[0], [out[1]], [out[1]])

        def carve(specs):
            off = 0
            d = {}
            for name, n in specs:
                d[name] = (arena[:, off:off + n], Region("ar_" + name))
                off += n
            assert off <= 14336, off
            return d

        try:
            for t in range(NTL):
                tok = slice(t * NT, (t + 1) * NT)
                for c in range(8):
                    k.dma("sp", xT[:, c, :], xT_d[c * 128:(c + 1) * 128, tok], (), [R_x[c]])
                for l in range(L):
                    k.barrier([R_ar])
                    A = carve([("t%d" % i, 512) for i in range(12)] +
                              [("Abd", 512), ("Bbd", 512), ("Kbd", 512), ("Vbd", 512), ("BT", 512), ("KT", 512), ("VT", 512),
                               ("Am", 512), ("ATm", 512), ("Tm", 512), ("Mka", 512), ("Mbr", 256), ("Mkr", 256),
                               ("preU", 256), ("U", 256), ("la", 256), ("rstd", 512)])
                    tmp = [A["t%d" % i] for i in range(12)]
                    rstd = A["rstd"]
                    rms_rstd(rstd, [xT[:, c, :] for c in range(8)], R_x, D, 1e-6, [tmp[0][0], tmp[1][0]], [tmp[0][1], tmp[1][1]])
                    for c in range(8):
                        stt("dve", tmp[c % 2][0], xT[:, c, :], dc(l, GM1, c), rstd[0], ALU.mult, ALU.mult,
                            [R_x[c], rstd[1], R_dcol], [tmp[c % 2][1]])
                        act(hT[:, c, :], tmp[c % 2][0], AF.Identity, [tmp[c % 2][1], R_dcol], [R_h], bias=dc(l, MOD, c))
                    if t == 0 and l == 0:
                        chk('mod', dcol[:, 0, :], [R_dcol])
                        chk('h', hT[:, 0, :], [R_h])
                    for pc in range(2):
                        wv_, Rw = wload(win_d[l, :, pc * 640:(pc + 1) * 640].rearrange("(c p) n -> p c n", p=128), [128, 8, 640])
                        for j in range(5):
                            oc = pc * 5 + j
                            bk, Rb = bank((0, 1))
                            for kc in range(8):
                                mm(bk[:], wv_[:, kc, j * 128:(j + 1) * 128], hT[:, kc, :], kc == 0, kc == 7, [Rw, R_h], [Rb])
                            cp("act", pT[:, oc, :], bk[:], [Rb], [R_big])
                    if t == 0 and l == 0:
                        chk('p0', pT[:, 0, :], [R_big])
                    for c in range(10):
                        tm_, Rt = tmp[c % 2]
                        ts("dve", tm_[:, 0:NT - 1], pT[:, c, 0:NT - 1], col(l, MU, c), None, ALU.mult, None, [R_big, R_cols], [Rt])
                        ts("dve", tm_[:, NT - 1:NT], pT[:, c, NT - 1:NT], 1.0, None, ALU.mult, None, [R_big], [Rt])
                        act(pT[:, c, :], pT[:, c, :], AF.Identity, [R_big, R_dcol], [R_big], scale=dc(l, OMU, c))
                        tt("dve", pT[:, c, 1:NT], pT[:, c, 1:NT], tm_[:, 0:NT - 1], ALU.add, [R_big, Rt], [R_big])
                        stt("dve", pT[:, c, 0:1], plast[:, l, c:c + 1], col(l, MU, c), pT[:, c, 0:1], ALU.mult, ALU.add,
                            [R_big, R_pl[l], R_cols], [R_big])
                        cp("dve", plast[:, l, c:c + 1], tm_[:, NT - 1:NT], [Rt], [R_pl[l]])
                    zT = pT
                    if t == 0 and l == 0:
                        chk('z0', pT[:, 0, :], [R_big])
                        chk('z9', pT[:, 9, :], [R_big])
                    la = A["la"][0].bitcast(BF16)
                    R_la = A["la"][1]
                    act(la[0:32, :], zT[0:32, 9, :], AF.Tanh, [R_big], [R_la])
                    act(la[64:128, :], zT[64:128, 9, :], AF.Sigmoid, [R_big], [R_la])
                    cp("dve", la[32:64, :], zT[32:64, 9, :], [R_big], [R_la])
                    for pr in range(3):
                        rT, kT_, vT_ = zT[:, pr, :], zT[:, 3 + pr, :], zT[:, 6 + pr, :]
                        sg, cum, Pin, Pinv, Pex, a_t, g_t, kk_, kp_, bon, Rt_, tx = tmp
                        bk, Rb = bank()
                        mm(bk[:], wl[:, l, pr * 128:(pr + 1) * 128], la, True, True, [R_SW[0], R_la], [Rb])
                        act(sg[0], bk[:], AF.Sigmoid, [Rb, R_cols], [sg[1]], bias=col(l, W0, pr))
                        k.op("dve", lambda e: e.tensor_tensor_scan(out=cum[0], data0=rmask, data1=sg[0], initial=0.0,
                                                                   op0=ALU.mult, op1=ALU.add), [sg[1], R_cst], [cum[1]])
                        act(Pin[0], cum[0], AF.Exp, [cum[1]], [Pin[1]], scale=-C0)
                        act(Pinv[0], cum[0], AF.Exp, [cum[1]], [Pinv[1]], scale=C0)
                        tt("dve", sg[0], cum[0], sg[0], ALU.subtract, [cum[1], sg[1]], [sg[1]])
                        act(Pex[0], sg[0], AF.Exp, [sg[1]], [Pex[1]], scale=-C0)
                        if t == 0 and l == 0 and pr == 0:
                            chk('cum', cum[0], [cum[1]])
                            chk('Pin', Pin[0], [Pin[1]])
                            chk('Pex', Pex[0], [Pex[1]])
                        bk, Rb = bank()
                        mm(bk[:], wl[:, l, 384 + pr * 128:384 + (pr + 1) * 128], la, True, True, [R_SW[0], R_la], [Rb])
                        act(a_t[0], bk[:], AF.Sigmoid, [Rb, R_cols], [a_t[1]], bias=col(l, A0, pr))
                        bk, Rb = bank()
                        mm(bk[:], wl[:, l, 768 + pr * 128:768 + (pr + 1) * 128], la, True, True, [R_SW[0], R_la], [Rb])
                        cp("act", g_t[0], bk[:], [Rb], [g_t[1]])
                        ts("dve", kk_[0], kT_, col(l, KK, pr), None, ALU.mult, None, [R_big, R_cols], [kk_[1]])
                        act(tx[0], kk_[0], AF.Square, [kk_[1]], [tx[1]])
                        bk, Rb = bank()
                        mm(bk[:], blk2, tx[0], True, True, [tx[1], R_cst], [Rb])
                        ts("dve", tx[0], bk[:], 1e-24, None, ALU.max, None, [Rb], [tx[1]])
                        act(tx[0], tx[0], AF.Sqrt, [tx[1]], [tx[1]])
                        recip(tx[0], tx[0], [tx[1]], [tx[1]])
                        tt("dve", kk_[0], kk_[0], tx[0], ALU.mult, [kk_[1], tx[1]], [kk_[1]])
                        if t == 0 and l == 0 and pr == 0:
                            chk('a', a_t[0], [a_t[1]])
                            chk('kk', kk_[0], [kk_[1]])
                        ts("dve", kp_[0], a_t[0], col(l, KA, pr), dc(l, OMKA, pr), ALU.mult, ALU.add, [a_t[1], R_cols, R_dcol], [kp_[1]])
                        tt("dve", kp_[0], kp_[0], kT_, ALU.mult, [kp_[1], R_big], [kp_[1]])
                        stt("dve", tx[0], rT, col(l, RK, pr), kp_[0], ALU.mult, ALU.mult, [R_big, kp_[1], R_cols], [tx[1]])
                        bk, Rb = bank()
                        mm(bk[:], blk2, tx[0], True, True, [tx[1], R_cst], [Rb])
                        tt("dve", bon[0], bk[:], vT_, ALU.mult, [Rb, R_big], [bon[1]])
                        if t == 0 and l == 0 and pr == 0:
                            chk('kp', kp_[0], [kp_[1]])
                            chk('bon', bon[0], [bon[1]])
                        tt("dve", Rt_[0], rT, Pin[0], ALU.mult, [R_big, Pin[1]], [Rt_[1]])
                        stt("dve", Pex[0], kk_[0], -1.0, Pex[0], ALU.mult, ALU.mult, [kk_[1], Pex[1]], [Pex[1]])
                        tt("dve", kk_[0], kk_[0], a_t[0], ALU.mult, [kk_[1], a_t[1]], [kk_[1]])
                        tt("dve", kk_[0], kk_[0], Pinv[0], ALU.mult, [kk_[1], Pinv[1]], [kk_[1]])
                        tt("dve", kp_[0], kp_[0], Pinv[0], ALU.mult, [kp_[1], Pinv[1]], [kp_[1]])
                        ybk, Rybk = banks[7], R_bk[7]
                        for half in range(2):
                            hs = slice(half * 256, (half + 1) * 256)
                            exp_ = {}
                            for nm, src, Rs in (("Abd", Pex[0], Pex[1]), ("Bbd", kk_[0], kk_[1]), ("Kbd", kp_[0], kp_[1]), ("Vbd", vT_, R_big)):
                                dst, Rd = A[nm]
                                d3 = dst.rearrange("p (c n) -> p c n", n=128)
                                if True:
                                    memset("pool", dst, 0.0, [Rd])
                                s3 = src[:, hs].rearrange("p (c n) -> p c n", n=64)
                                cp("pool", d3[0:64, :, 0:64], s3[0:64], [Rs], [Rd])
                                cp("pool", d3[64:128, :, 64:128], s3[64:128], [Rs], [Rd])
                                exp_[nm] = (d3, Rd)
                            for nm, src in (("BT", "Bbd"), ("KT", "Kbd"), ("VT", "Vbd")):
                                bk, Rb = bank()
                                for c in range(4):
                                    k.op("pe", lambda e, c=c, bk=bk, src=src: e.transpose(bk[:, c * 128:(c + 1) * 128], exp_[src][0][:, c, :], ident),
                                         [exp_[src][1], R_cst], [Rb])
                                cp("act", A[nm][0], bk[:], [Rb], [A[nm][1]])
                            BT3 = A["BT"][0].rearrange("p (c n) -> p c n", n=128)
                            KT3 = A["KT"][0].rearrange("p (c n) -> p c n", n=128)
                            VT3 = A["VT"][0].rearrange("p (c n) -> p c n", n=128)
                            Abd3, Bbd3, Kbd3 = exp_["Abd"][0], exp_["Bbd"][0], exp_["Kbd"][0]
                            R_Abd, R_Bbd, R_Kbd = exp_["Abd"][1], exp_["Bbd"][1], exp_["Kbd"][1]
                            su4 = su2.unsqueeze(1).to_broadcast([128, 4, 128])
                            sl4 = sl2.unsqueeze(1).to_broadcast([128, 4, 128])
                            ui4 = uis.unsqueeze(1).to_broadcast([128, 4, 64])

                            def gram(dst, lhs3, Rl, rhs3, Rr, mask4, n):
                                bk, Rb = bank()
                                for c in range(4):
                                    mm(bk[:, c * n:(c + 1) * n], lhs3[:, c, :], rhs3[:, c, :], True, True, [Rl, Rr], [Rb])
                                tt("dve", dst[0].rearrange("p (c n) -> p c n", n=n), bk[:, 0:4 * n].rearrange("p (c n) -> p c n", n=n),
                                   mask4, ALU.mult, [Rb, R_cst], [dst[1]])

                            gram(A["Am"], Bbd3, R_Bbd, Abd3, R_Abd, su4, 128)
                            gram(A["ATm"], Abd3, R_Abd, Bbd3, R_Bbd, sl4, 128)
                            gram(A["Mka"], Kbd3, R_Kbd, Abd3, R_Abd, su4, 128)
                            Rt3 = Rt_[0][:, hs].rearrange("p (c n) -> p c n", n=64)
                            gram(A["Mbr"], Bbd3, R_Bbd, Rt3, Rt_[1], ui4, 64)
                            gram(A["Mkr"], Kbd3, R_Kbd, Rt3, Rt_[1], ui4, 64)
                            Am3 = A["Am"][0].rearrange("p (c n) -> p c n", n=128)
                            ATm3 = A["ATm"][0].rearrange("p (c n) -> p c n", n=128)
                            Tm3 = A["Tm"][0].rearrange("p (c n) -> p c n", n=128)
                            Mka3 = A["Mka"][0].rearrange("p (c n) -> p c n", n=128)
                            Mbr3 = A["Mbr"][0].rearrange("p (c n) -> p c n", n=64)
                            Mkr3 = A["Mkr"][0].rearrange("p (c n) -> p c n", n=64)
                            R_Am, R_ATm, R_Tm = A["Am"][1], A["ATm"][1], A["Tm"][1]
                            tt("dve", Tm3, Am3, ident.unsqueeze(1).to_broadcast([128, 4, 128]), ALU.add, [R_Am, R_cst], [R_Tm])
                            for j in range(1, 6):
                                bkA, RbA = bank()
                                bkT, RbT = bank()
                                if j < 5:
                                    for c in range(4):
                                        mm(bkA[:, c * 128:(c + 1) * 128], ATm3[:, c, :], Am3[:, c, :], True, True, [R_Am, R_ATm], [RbA])
                                for c in range(4):
                                    mm(bkT[:, c * 128:(c + 1) * 128], Am3[:, c, :], ATm3[:, c, :], True, True, [R_Am, R_ATm], [RbT])
                                if j < 5:
                                    cp("act", A["Am"][0], bkA[:], [RbA], [R_Am])
                                cp("dve", A["ATm"][0], bkT[:], [RbT], [R_ATm])
                                bkU, RbU = bank()
                                for c in range(4):
                                    mm(bkU[:, c * 128:(c + 1) * 128], ATm3[:, c, :], Tm3[:, c, :], True, True, [R_ATm, R_Tm], [RbU])
                                tt("dve", A["Tm"][0], A["Tm"][0], bkU[:], ALU.add, [R_Tm, RbU], [R_Tm])
                            S_, RS_ = Sbd[l][pr], R_S[l][pr]
                            preU, U_ = A["preU"], A["U"]
                            for c in range(4):
                                cg = half * 4 + c
                                bk, Rb = bank()
                                mm(bk[:, 0:128], Abd3[:, c, :], S_[:], True, False, [R_Abd, RS_], [Rb])
                                mm(bk[:, 0:128], Mka3[:, c, :], VT3[:, c, :], False, True, [A["Mka"][1], A["VT"][1]], [Rb])
                                cp("act", preU[0][:, 0:128], bk[:, 0:128], [Rb], [preU[1]])
                                bk2, Rb2 = bank()
                                mm(bk2[:, 0:128], Tm3[:, c, :], preU[0][:, 0:128], True, True, [R_Tm, preU[1]], [Rb2])
                                cp("dve", U_[0][:, 0:128], bk2[:, 0:128], [Rb2], [U_[1]])
                                yc = ybk[:, cg * 64:(cg + 1) * 64]
                                mm(yc, S_[:], Rt_[0][:, cg * 64:(cg + 1) * 64], True, False, [RS_, Rt_[1]], [Rybk])
                                mm(yc, U_[0][:, 0:128], Mbr3[:, c, :], False, False, [U_[1], A["Mbr"][1]], [Rybk])
                                mm(yc, VT3[:, c, :], Mkr3[:, c, :], False, True, [A["VT"][1], A["Mkr"][1]], [Rybk])
                                bk3, Rb3 = bank()
                                mm(bk3[:, 0:128], ident, S_[:], True, False, [R_cst, RS_], [Rb3])
                                mm(bk3[:, 0:128], BT3[:, c, :], U_[0][:, 0:128], False, False, [A["BT"][1], U_[1]], [Rb3])
                                mm(bk3[:, 0:128], KT3[:, c, :], VT3[:, c, :], False, True, [A["KT"][1], A["VT"][1]], [Rb3])
                                ts("dve", S_[:], bk3[:, 0:128], Pin[0][:, cg * 64 + 63:cg * 64 + 64], None, ALU.mult, None,
                                   [Rb3, Pin[1]], [RS_])
                        ysb = sg
                        cp("act", ysb[0], ybk[:], [Rybk], [ysb[1]])
                        if t == 0 and l == 0 and pr == 0:
                            chk('y', ysb[0], [ysb[1]])
                        bk, Rb = bank()
                        mm(bk[:], blk2, ysb[0], True, True, [ysb[1], R_cst], [Rb])
                        stt("dve", ysb[0], bk[:], -1.0 / 64, ysb[0], ALU.mult, ALU.add, [Rb, ysb[1]], [ysb[1]])
                        act(tx[0], ysb[0], AF.Square, [ysb[1]], [tx[1]])
                        bk, Rb = bank()
                        mm(bk[:], blk2, tx[0], True, True, [tx[1], R_cst], [Rb])
                        act(tx[0], bk[:], AF.Sqrt, [Rb], [tx[1]], bias=64e-5, scale=1.0 / 64)
                        recip(tx[0], tx[0], [tx[1]], [tx[1]])
                        tt("dve", ysb[0], ysb[0], tx[0], ALU.mult, [ysb[1], tx[1]], [ysb[1]])
                        ts("dve", ysb[0], ysb[0], col(l, LNW, pr), col(l, LNB, pr), ALU.mult, ALU.add, [ysb[1], R_cols], [ysb[1]])
                        tt("dve", ysb[0], ysb[0], bon[0], ALU.add, [ysb[1], bon[1]], [ysb[1]])
                        tt("dve", mixT[:, pr, :], ysb[0], g_t[0], ALU.mult, [ysb[1], g_t[1]], [R_mix[pr]])
                        if t == 0 and l == 0 and pr == 0:
                            chk('ya', mixT[:, 0, :], [R_mix[0]])

                    k.barrier([R_ar])
                    Bm = carve([("f%d" % i, 512) for i in range(5)] + [("acc0", 512), ("acc1", 512), ("rq", 512), ("rkv", 512),
                               ("qn", 512), ("kvn", 512), ("qT", 1536), ("kTn", 1536), ("vnew", 768),
                               ("kb0", 512), ("kb1", 512), ("vb0", 512), ("vb1", 512), ("pt0", 256), ("pt1", 256), ("pt2", 256),
                               ("osb", 512), ("rde", 512), ("rdo", 512)])
                    ft = [Bm["f%d" % i] for i in range(5)]
                    for pc, (c0_, ncol, nch) in enumerate(((1280, 640, 5), (1920, 480, 4))):
                        wv_, Rw = wload(win_d[l, :, c0_:c0_ + ncol].rearrange("(c p) n -> p c n", p=128), [128, 8, ncol])
                        for j in range(nch):
                            oc = pc * 5 + j
                            M = 96 if oc == 8 else 128
                            bk, Rb = bank((0, 1))
                            for kc in range(8):
                                mm(bk[0:M, :], wv_[:, kc, j * 128:j * 128 + M], hT[:, kc, :], kc == 0, kc == 7, [Rw, R_h], [Rb])
                            cp("act", pT[0:M, oc, :], bk[0:M, :], [Rb], [R_big])
                    hb, Rhb = hbuf[l], R_hb[l]
                    for ch in range(2):
                        act(ft[0][0], pT[:, 2 + ch, :], AF.Sigmoid, [R_big], [ft[0][1]])
                        tt("dve", hb[:, ch, 30:30 + NT], pT[:, ch, :], ft[0][0], ALU.mult, [R_big, ft[0][1]], [Rhb])
                    accs = [Bm["acc0"], Bm["acc1"]]
                    for ch in range(2):
                        en = "pool" if ch == 1 else "dve"
                        acc, Racc = accs[ch]
                        ts(en, acc, hb[:, ch, 0:NT], col(l, CW, ch), col(l, CB, ch), ALU.mult, ALU.add, [Rhb, R_cols], [Racc])
                        for j in range(1, 31):
                            if en == "dve":
                                stt(en, acc, hb[:, ch, j:j + NT], col(l, CW, j * 2 + ch), acc, ALU.mult, ALU.add, [Rhb, Racc, R_cols], [Racc])
                            else:
                                ts(en, ft[4][0], hb[:, ch, j:j + NT], col(l, CW, j * 2 + ch), None, ALU.mult, None, [Rhb, R_cols], [ft[4][1]])
                                tt(en, acc, acc, ft[4][0], ALU.add, [Racc, ft[4][1]], [Racc])
                    for ch in range(2):
                        cp("act", hb[:, ch, 0:30], hb[:, ch, NT:NT + 30], [Rhb] + [a[1] for a in accs], [Rhb])
                    bk, Rb = bank()
                    for ch in range(2):
                        mm(bk[:], ones_m, accs[ch][0], ch == 0, ch == 1, [accs[ch][1], R_cst], [Rb])
                    for ch in range(2):
                        stt("dve", accs[ch][0], bk[:], -1.0 / 256, accs[ch][0], ALU.mult, ALU.add, [Rb, accs[ch][1]], [accs[ch][1]])
                    rms_rstd(ft[2], [accs[0][0], accs[1][0]], [accs[0][1], accs[1][1]], 256, 1e-5, [ft[0][0], ft[1][0]], [ft[0][1], ft[1][1]])
                    for ch in range(2):
                        tt("dve", accs[ch][0], accs[ch][0], ft[2][0], ALU.mult, [accs[ch][1], ft[2][1]], [accs[ch][1]])
                        ts("dve", accs[ch][0], accs[ch][0], col(l, CLW, ch), col(l, CLB, ch), ALU.mult, ALU.add, [accs[ch][1], R_cols], [accs[ch][1]])
                        act(mixT[:, 3 + ch, :], accs[ch][0], AF.Silu, [accs[ch][1]], [R_mix[3 + ch]])
                    if t == 0 and l == 0:
                        chk('yb', mixT[:, 3, :], [R_mix[3]])
                    if l == 0:
                        k.dma("pool", ft[4][0][64:96, :], pos_d[:, tok].partition_broadcast(32).rearrange("p o t -> p (o t)"), (), [ft[4][1]])
                        ang = ft[4][0][64:96, :]
                        ts("dve", ang, ang, ifr[64:96, :], None, ALU.mult, None, [ft[4][1], R_cst], [ft[4][1]])
                        chk('ang0', ang, [ft[4][1]])
                        kf = ft[3][0][64:96, :]
                        ki = kint[64:96, :]
                        ts("dve", kf, ang, float(1.0 / (2 * np.pi)), None, ALU.mult, None, [ft[4][1]], [ft[3][1]])
                        cp("dve", ki, kf, [ft[3][1]], [ft[2][1]])
                        cp("dve", kf, ki, [ft[2][1]], [ft[3][1]])
                        chk('kf', kf, [ft[3][1]])
                        for cc in (6.28125, 1.9350051879882812e-03, 3.0199159819567e-07):
                            stt("dve", ang, kf, -cc, ang, ALU.mult, ALU.add, [ft[3][1], ft[4][1]], [ft[4][1]])
                        ts("dve", kf, ang, float(np.pi), float(-2 * np.pi), ALU.is_gt, ALU.mult, [ft[4][1]], [ft[3][1]])
                        tt("dve", ang, ang, kf, ALU.add, [ft[4][1], ft[3][1]], [ft[4][1]])
                        ts("dve", kf, ang, float(-np.pi), float(2 * np.pi), ALU.is_lt, ALU.mult, [ft[4][1]], [ft[3][1]])
                        tt("dve", ang, ang, kf, ALU.add, [ft[4][1], ft[3][1]], [ft[4][1]])
                        chk('red', ang, [ft[4][1]])
                        act(sinT[64:96, :], ang, AF.Sin, [ft[4][1]], [R_cos])
                        act(kf, ang, AF.Sin, [ft[4][1]], [ft[3][1]], scale=0.5)
                        tt("dve", kf, kf, kf, ALU.mult, [ft[3][1]], [ft[3][1]])
                        ts("dve", cosT[64:96, :], kf, -2.0, 1.0, ALU.mult, ALU.add, [ft[3][1]], [R_cos])
                        if t == 0:
                            chk('cos', cosT[64:96, :], [R_cos])
                            chk('sin', sinT[64:96, :], [R_cos])
                    rq, rkv = Bm["rq"], Bm["rkv"]
                    rms_rstd(rq, [pT[:, 4, :], pT[:, 5, :]], [R_big], 256, 1e-6, [ft[0][0], ft[1][0]], [ft[0][1], ft[1][1]])
                    rms_rstd(rkv, [pT[:, 6, :], pT[:, 7, :]], [R_big], 256, 1e-6, [ft[0][0], ft[1][0]], [ft[0][1], ft[1][1]])
                    qn = Bm["qn"][0].bitcast(BF16).rearrange("p (c t) -> p c t", t=NT)
                    kvn = Bm["kvn"][0].bitcast(BF16).rearrange("p (c t) -> p c t", t=NT)
                    for c in range(2):
                        stt("dve", qn[:, c, :], pT[:, 4 + c, :], col(l, QN, c), rq[0], ALU.mult, ALU.mult, [R_big, rq[1], R_cols], [Bm["qn"][1]])
                        stt("dve", kvn[:, c, :], pT[:, 6 + c, :], col(l, KVN, c), rkv[0], ALU.mult, ALU.mult, [R_big, rkv[1], R_cols], [Bm["kvn"][1]])
                    qT = Bm["qT"][0].bitcast(BF16).rearrange("p (h t) -> p h t", t=NT)
                    kTn = Bm["kTn"][0].bitcast(BF16).rearrange("p (h t) -> p h t", t=NT)
                    vnew = Bm["vnew"][0].bitcast(BF16).rearrange("p (b n) -> p b n", n=384)
                    R_qT, R_kTn, R_vnew = Bm["qT"][1], Bm["kTn"][1], Bm["vnew"][1]

                    def rope(dst, raw, Rraw, Rdst):
                        bk, Rb = bank()
                        mm(bk[0:96, :], rotm[0:96, 0:96], raw[0:96, :], True, True, [Rraw, R_cst], [Rb])
                        tt("dve", ft[1][0][64:96, :], bk[64:96, :], sinT[64:96, :], ALU.mult, [Rb, R_cos], [ft[1][1]])
                        tt("dve", ft[2][0][64:96, :], raw[64:96, :], cosT[64:96, :], ALU.mult, [Rraw, R_cos], [ft[2][1]])
                        tt("dve", dst, ft[1][0][64:96, :], ft[2][0][64:96, :], ALU.add, [ft[1][1], ft[2][1]], [Rdst])

                    kpe, Rkpe = ft[3]
                    rope(kpe.bitcast(BF16)[64:96, 0:NT], pT[:, 8, :], R_big, Rkpe)
                    for h in range(6):
                        bk, Rb = bank()
                        for c in range(2):
                            mm(bk[0:96, :], wuq[:, l, c, h * 96:(h + 1) * 96], qn[:, c, :], c == 0, c == 1, [R_SW[1], Bm["qn"][1]], [Rb])
                        cp("act", ft[0][0][0:96, :], bk[0:96, :], [Rb], [ft[0][1]])
                        cp("act", qT[0:64, h, :], bk[0:64, :], [Rb], [R_qT])
                        rope(qT[64:96, h, :], ft[0][0], ft[0][1], R_qT)
                        bk, Rb = bank()
                        for c in range(2):
                            mm(bk[0:64, :], wk[:, l, c, h * 64:(h + 1) * 64], kvn[:, c, :], c == 0, c == 1, [R_SW[2], Bm["kvn"][1]], [Rb])
                        cp("act", kTn[0:64, h, :], bk[0:64, :], [Rb], [R_kTn])
                        cp("dve", kTn[64:96, h, :], kpe.bitcast(BF16)[64:96, 0:NT], [Rkpe], [R_kTn])
                    for b4 in range(4):
                        bk, Rb = bank()
                        for c in range(2):
                            mm(bk[:, 0:384], kvn[:, c, b4 * 128:(b4 + 1) * 128], wv[:, l, c, :], c == 0, c == 1, [R_SW[3], Bm["kvn"][1]], [Rb])
                        cp("act", vnew[:, b4, :], bk[:, 0:384], [Rb], [R_vnew])
                    if t == 0 and l == 0:
                        chk('q0', qT[0:96, 0, :], [R_qT])
                        chk('k0', kTn[0:96, 0, :], [R_kTn])
                        chk('v0', vnew[:, 0, :], [R_vnew])
                    for h in range(6):
                        k.dma("sp", kc_d[l, h, :, tok], kTn[0:96, h, :], [R_kTn], [R_kc[l][h]])
                        k.dma("sp", vc_d[l, h, :, t * 4:(t + 1) * 4, :], vnew[:, :, h * 64:(h + 1) * 64], [R_vnew], [R_vc[l][h]])
                    kbs = [Bm["kb0"], Bm["kb1"]]
                    vbs = [Bm["vb0"], Bm["vb1"]]
                    pts = [Bm["pt0"], Bm["pt1"], Bm["pt2"]]
                    osb = Bm["osb"]
                    rds = [Bm["rde"], Bm["rdo"]]
                    for i in range(2):
                        memset("pool", vbs[i][0].bitcast(BF16), 1.0, [vbs[i][1]])
                        memset("pool", rds[i][0], 0.0, [rds[i][1]])
                    nseg = t + 1
                    scale = float(96 ** -0.5)
                    li = 0
                    pi = 0
                    for h in range(6):
                        obk, Robk = bank((0, 1))
                        par = h % 2
                        first = True
                        for s in range(nseg):
                            kb, Rkb = kbs[li % 2]
                            vb, Rvb = vbs[li % 2]
                            li += 1
                            kbv = kb.bitcast(BF16)[0:96, 0:NT]
                            vbv = vb.bitcast(BF16).rearrange("p (b n) -> p b n", n=128)
                            k.dma("sp", kbv, kc_d[l, h, :, s * NT:(s + 1) * NT], [R_kc[l][h]], [Rkb])
                            if par == 0:
                                k.dma("sp", vbv[:, 0:4, 0:64], vc_d[l, h, :, s * 4:(s + 1) * 4, :], [R_vc[l][h]], [Rvb])
                            else:
                                k.dma("sp", vbv[:, 4:8, 64:128], vc_d[l, h, :, s * 4:(s + 1) * 4, :], [R_vc[l][h]], [Rvb])
                            for jb in range(4):
                                diag = (s == t)
                                q0 = jb * 128 if diag else 0
                                nq = NT - q0
                                sbk, Rsb = bank()
                                mm(sbk[:, 0:nq], kbv[:, jb * 128:(jb + 1) * 128], qT[0:96, h, q0:NT], True, True, [Rkb, R_qT], [Rsb])
                                pt, Rpt = pts[pi % 3]
                                pi += 1
                                ptv = pt.bitcast(BF16)[:, 0:NT]
                                act(ptv[:, 0:nq], sbk[:, 0:nq], AF.Exp, [Rsb], [Rpt], scale=scale)
                                if diag:
                                    tt("pool", ptv[:, 0:128], ptv[:, 0:128], cmb, ALU.mult, [Rpt, R_cst], [Rpt])
                                last = (s == nseg - 1) and (jb == 3)
                                vsl = vbv[:, jb, :] if par == 0 else vbv[:, 4 + jb, :]
                                mm(obk[:, q0:NT], vsl, ptv[:, 0:nq], first, last, [Rvb, Rpt], [Robk])
                                first = False
                        olo, dlo = (0, 64) if par == 0 else (64, 0)
                        cp("act", osb[0][olo:olo + 64, :], obk[olo:olo + 64, :], [Robk], [osb[1]])
                        rd, Rrd = rds[par]
                        recip(rd[dlo:dlo + 64, :], obk[dlo:dlo + 64, :], [Robk], [Rrd])
                        bk, Rb = bank()
                        mm(bk[:], swapm, rd, True, True, [Rrd, R_cst], [Rb])
                        tt("dve", mixT[olo:olo + 64, 5 + h // 2, :], osb[0][olo:olo + 64, :], bk[olo:olo + 64, :], ALU.mult,
                           [osb[1], Rb], [R_mix[5 + h // 2]])
                    if t == 0 and l == 0:
                        chk('yc', mixT[:, 5, :], [R_mix[5]])

                    k.barrier([R_ar])
                    Cm = carve([("y", 4096), ("rs", 512), ("s0", 512), ("s1", 512), ("u0", 512), ("u1", 512)])
                    yT = Cm["y"][0].rearrange("p (c t) -> p c t", t=NT)
                    R_y = Cm["y"][1]

                    def resid(gbase):
                        rms_rstd(Cm["rs"], [yT[:, c, :] for c in range(8)], [R_y], D, 1e-6, [Cm["s0"][0], Cm["s1"][0]], [Cm["s0"][1], Cm["s1"][1]])
                        for c in range(8):
                            u, Ru = Cm["u%d" % (c % 2)]
                            stt("dve", u, yT[:, c, :], dc(l, gbase, c), Cm["rs"][0], ALU.mult, ALU.mult, [R_y, Cm["rs"][1], R_dcol], [Ru])
                            tt("pool", xT[:, c, :], xT[:, c, :], u, ALU.add, [R_x[c], Ru], [R_x[c]])

                    wv_, Rw = wload(wout_d[l].rearrange("(c p) n -> p c n", p=128), [128, 8, 1024])
                    for oc in range(8):
                        bk, Rb = bank((0, 1))
                        for kc in range(8):
                            mm(bk[:], wv_[:, kc, oc * 128:(oc + 1) * 128], mixT[:, kc, :], kc == 0, kc == 7, [Rw, R_mix[kc]], [Rb])
                        cp("act", yT[:, oc, :], bk[:], [Rb], [R_y])
                    resid(GG1)
                    if t == 0 and l == 0:
                        chk('x1', xT[:, 0, :], [R_x[0]])
                    rms_rstd(Cm["rs"], [xT[:, c, :] for c in range(8)], R_x, D, 1e-6, [Cm["s0"][0], Cm["s1"][0]], [Cm["s0"][1], Cm["s1"][1]])
                    for c in range(8):
                        u, Ru = Cm["u%d" % (c % 2)]
                        stt("dve", u, xT[:, c, :], dc(l, GM2, c), Cm["rs"][0], ALU.mult, ALU.mult, [R_x[c], Cm["rs"][1], R_dcol], [Ru])
                        act(hT[:, c, :], u, AF.Identity, [Ru, R_dcol], [R_h], bias=dc(l, MOD + 24, c))
                    for pc in range(4):
                        wv_, Rw = wload(ff1_d[l, :, pc * 1024:(pc + 1) * 1024].rearrange("(c p) n -> p c n", p=128), [128, 8, 1024])
                        for j in range(8):
                            hc = pc * 8 + j
                            bk, Rb = bank((0, 1))
                            for kc in range(8):
                                mm(bk[:], wv_[:, kc, j * 128:(j + 1) * 128], hT[:, kc, :], kc == 0, kc == 7, [Rw, R_h], [Rb])
                            s_, Rs_ = Cm["s%d" % (hc % 2)]
                            act(s_, bk[:], AF.Square, [Rb], [Rs_])
                            stt("dve", hid[:, hc, :], bk[:], 0.0, s_, ALU.is_gt, ALU.mult, [Rb, Rs_], [R_big])
                    for pc in range(4):
                        wv_, Rw = wload(ff2_d[l, :, pc * 256:(pc + 1) * 256].rearrange("(c p) n -> p c n", p=128), [128, 32, 256])
                        for j in range(2):
                            oc = pc * 2 + j
                            bk, Rb = bank((0, 1))
                            for kc in range(32):
                                mm(bk[:], wv_[:, kc, j * 128:(j + 1) * 128], hid[:, kc, :], kc == 0, kc == 31, [Rw, R_big], [Rb])
                            cp("act", yT[:, oc, :], bk[:], [Rb], [R_y])
                    resid(GG2)
                    if t == 0 and l == 0:
                        chk('x2', xT[:, 0, :], [R_x[0]])
                for c in range(8):
                    k.dma("sp", oT_d[c * 128:(c + 1) * 128, tok], xT[:, c, :], [R_x[c]], [R_out], accumulate=True)
        except _Stop:
            pass
        k.finish([R_out, R_dbg])
        print("instructions emitted:", k.nins, "sems:", k.nsem)
    _DBG_SLOTS.clear()
    _DBG_SLOTS.update(dbg_slots)
    return nc


def _consts():
    c = np.zeros((128, NCONST), np.float32)
    c[:, K_ID:K_ID + 128] = np.eye(128)
    c[:, K_ONES:K_ONES + 128] = 1.0
    i = np.arange(128)
    same = (i[:, None] // 64) == (i[None, :] // 64)
    c[:, K_BLK:K_BLK + 128] = same
    li = i % 64
    c[:, K_SU:K_SU + 128] = same & (li[:, None] < li[None, :])
    c[:, K_SL:K_SL + 128] = same & (li[:, None] > li[None, :])
    c[:, K_UIS:K_UIS + 64] = li[:, None] <= np.arange(64)[None, :]
    c[:, K_CM:K_CM + 128] = i[:, None] <= i[None, :]
    rm = np.ones(512, np.float32)
    rm[::64] = 0.0
    c[:, K_RM:K_RM + 512] = rm[None, :]
    sw = np.zeros((128, 128), np.float32)
    sw[(i + 64) % 128, i] = 1.0
    c[:, K_SWAP:K_SWAP + 128] = sw
    rot = np.zeros((128, 128), np.float32)
    for r in range(16):
        rot[64 + r + 16, 64 + r] = -1.0
        rot[64 + r, 64 + r + 16] = 1.0
    c[:, K_ROT:K_ROT + 128] = rot
    inv = (1.0 / (np.float32(10000.0) ** (np.arange(0, 32, 2, dtype=np.float32) / np.float32(32)))).astype(np.float32)
    c[64:96, K_IF] = np.concatenate([inv, inv])
    return c


def _colize(v):
    v = np.asarray(v, np.float32).reshape(-1, 128)
    return np.ascontiguousarray(v.T)


def _prep(inp, T):
    f = lambda a: np.ascontiguousarray(np.asarray(a, np.float32))
    cols = np.zeros((L, 128, NCOL), np.float32)
    wl = np.zeros((L, 128, 1152), np.float32)
    for l in range(L):
        def put(base, v):
            cc = _colize(v)
            cols[l, :, base:base + cc.shape[1]] = cc
        put(GPM, inp["g_pre_mix"][l]); put(GQM, inp["g_post_mix"][l]); put(GPF, inp["g_pre_ffn"][l]); put(GQF, inp["g_post_ffn"][l])
        put(BADA, inp["b_ada"][l])
        put(MU, inp["rwkv_mu"][l]); put(W0, inp["rwkv_w0"][l]); put(A0, inp["rwkv_a0"][l]); put(KK, inp["rwkv_k_k"][l])
        put(KA, inp["rwkv_k_a"][l]); put(RK, np.asarray(inp["rwkv_r_k"][l]).reshape(-1)); put(LNW, inp["rwkv_ln_w"][l])
        put(LNB, inp["rwkv_ln_b"][l]); put(CB, inp["conv_b"][l]); put(CLW, inp["conv_ln_w"][l]); put(CLB, inp["conv_ln_b"][l])
        put(QN, inp["mla_q_norm"][l]); put(KVN, inp["mla_kv_norm"][l])
        cw = np.asarray(inp["conv_w"][l], np.float32)
        for j in range(31):
            for ch in range(2):
                cols[l, :, CW + j * 2 + ch] = cw[j, ch * 128:(ch + 1) * 128]
        wl[l, 0:32, 0:384] = inp["rwkv_w2"][l]
        wl[l, 32:64, 384:768] = inp["rwkv_a2"][l]
        wl[l, 64:128, 768:1152] = inp["rwkv_g2"][l]
    w_in = np.asarray(inp["w_in"], np.float32)
    win = np.concatenate([w_in[:, :, 0:2304], w_in[:, :, 2240:2304], w_in[:, :, 2304:2336]], axis=2)
    wukv = np.asarray(inp["mla_w_ukv"], np.float32).reshape(L, 256, 6, 128)
    wk = np.ascontiguousarray(wukv[:, :, :, 0:64].reshape(L, 256, 384))
    wv = np.ascontiguousarray(wukv[:, :, :, 64:128].reshape(L, 256, 384))
    shared = {
        "wada": f(inp["w_ada"]), "win": f(win), "wout": f(inp["w_out"]), "ff1": f(inp["w_ff1"]), "ff2": f(inp["w_ff2"]),
        "wl": wl, "wuq": f(inp["mla_w_uq"]), "wk": wk, "wv": wv, "cols": cols, "consts": _consts(),
    }
    maps = []
    x = np.asarray(inp["x"], np.float32)
    c = np.asarray(inp["c"], np.float32)
    pos = np.asarray(inp["positions"], np.int32)
    nb = x.shape[0]
    for core in range(8):
        b = core % nb
        m = dict(shared)
        m["xT"] = np.ascontiguousarray(x[b, :T].T)
        m["cT"] = _colize(c[b])
        m["pos"] = np.ascontiguousarray(pos[b:b + 1, :T])
        maps.append(m)
    return maps


_NC_CACHE = {}
_DBG_SLOTS = {}


def run_dbg(inp, T, names, stop):
    nc = build(T, dbg=(names, stop))
    maps = _prep(inp, T)
    res = run_bass_kernel_spmd(nc, maps, core_ids=list(range(8)))
    return res.results[0]["dbg"], dict(_DBG_SLOTS)


def run(inp, T):
    if T not in _NC_CACHE:
        _NC_CACHE[T] = build(T)
    nc = _NC_CACHE[T]
    maps = _prep(inp, T)
    res = run_bass_kernel_spmd(nc, maps, core_ids=list(range(8)))
    nb = np.asarray(inp["x"]).shape[0]
    out = np.stack([np.ascontiguousarray(res.results[b]["oT"].T) for b in range(nb)], axis=0)
    return out.astype(np.float32)


def kernel(**inputs):
    T = np.asarray(inputs["x"]).shape[1]
    return run(inputs, T)
```

```python
import os
import numpy as np
from contextlib import ExitStack
import concourse.bass as bass
import concourse.mybir as mybir
from concourse.bass_utils import run_bass_kernel_spmd

F32 = mybir.dt.float32
BF16 = mybir.dt.bfloat16
I32 = mybir.dt.int32
AF = mybir.ActivationFunctionType
ALU = mybir.AluOpType

D = 1024
NT = 512
L = 2
C0 = float(np.exp(-0.5))
GPM, GQM, GPF, GQF, BADA, MU, W0, A0, KK, KA, RK, LNW, LNB, CB, CLW, CLB, QN, KVN, CW = (
    0, 8, 16, 24, 32, 80, 90, 93, 96, 99, 102, 105, 108, 111, 113, 115, 117, 119, 121)
NCOL = 121 + 62
OMU, OMKA, MOD, GM1, GG1, GM2, GG2, NDC = 0, 10, 13, 61, 69, 77, 85, 93
K_ID, K_ONES, K_BLK, K_SU, K_SL, K_UIS, K_CM, K_RM, K_SWAP, K_ROT, K_IF, NCONST = (
    0, 128, 256, 384, 512, 640, 704, 832, 1344, 1472, 1600, 1601)


class Region:
    __slots__ = ("name", "w", "r", "dsem", "dcnt")

    def __init__(self, name):
        self.name = name
        self.w = None
        self.r = {}
        self.dsem = None
        self.dcnt = 0


class Eng:
    def __init__(self, name, obj, sem):
        self.name = name
        self.obj = obj
        self.sem = sem
        self.cnt = 0
        self.seen = {}


class KB:
    def __init__(self, nc, es):
        self.nc = nc
        self.es = es
        self.nsem = 0
        self.E = {}
        for n, o in (("pe", nc.tensor), ("act", nc.scalar), ("dve", nc.vector), ("pool", nc.gpsimd), ("sp", nc.sync)):
            self.E[n] = Eng(n, o, self.newsem("e_" + n))
        self.nins = 0
        self.dsems = {}

    def newsem(self, name):
        self.nsem += 1
        return self.es.enter_context(self.nc.semaphore(name + "_%d" % self.nsem))

    def _waits(self, e, reads, writes, skip_dma_waw=None):
        need = {}

        def add(ev):
            if ev is None:
                return
            sem, val = ev
            if sem is e.sem:
                return
            if e.seen.get(id(sem), 0) >= val:
                return
            if need.get(id(sem), (None, 0))[1] < val:
                need[id(sem)] = (sem, val)

        for R in reads:
            add(R.w)
        for R in writes:
            if not (skip_dma_waw is not None and R.w is not None and R.w[0] is skip_dma_waw):
                add(R.w)
            for ev in R.r.values():
                add(ev)
        for sem, val in need.values():
            e.obj.wait_ge(sem, val)
            e.seen[id(sem)] = val

    def op(self, en, fn, reads=(), writes=(), inc=True):
        e = self.E[en]
        self._waits(e, reads, writes)
        ins = fn(e.obj)
        if inc:
            e.cnt += 1
            ins.then_inc(e.sem, 1)
            ev = (e.sem, e.cnt)
        else:
            ev = (e.sem, e.cnt + 1)
        for R in reads:
            R.r[id(e.sem)] = ev
        for R in writes:
            R.w = ev
            R.r = {}
        self.nins += 1
        return ins

    def dma(self, qn, out, in_, reads=(), writes=(), accumulate=False):
        q = self.E[qn]
        W = writes[0]
        if W.name not in self.dsems:
            self.dsems[W.name] = [self.newsem("d_" + W.name), 0]
        ds = self.dsems[W.name]
        W.dsem = ds[0]
        self._waits(q, reads, writes, skip_dma_waw=W.dsem if accumulate else None)
        ins = q.obj.dma_start(out=out, in_=in_)
        ds[1] += 16
        ins.then_inc(W.dsem, 16)
        ev = (W.dsem, ds[1])
        for R in reads:
            R.r[id(W.dsem)] = ev
        for R in writes:
            R.w = ev
            R.r = {}
        self.nins += 1

    def barrier(self, regions=()):
        ce = ["pe", "act", "dve", "pool"]
        for a in ce:
            ea = self.E[a]
            for b in ce:
                if a == b:
                    continue
                eb = self.E[b]
                if eb.cnt > 0 and ea.seen.get(id(eb.sem), 0) < eb.cnt:
                    ea.obj.wait_ge(eb.sem, eb.cnt)
                    ea.seen[id(eb.sem)] = eb.cnt
            for R in regions:
                evs = list(R.r.values()) + ([R.w] if R.w else [])
                for sem, val in evs:
                    if sem is ea.sem or ea.seen.get(id(sem), 0) >= val:
                        continue
                    ea.obj.wait_ge(sem, val)
                    ea.seen[id(sem)] = val

    def finish(self, regions):
        e = self.E["sp"]
        for R in regions:
            if R.w is not None:
                sem, val = R.w
                if e.seen.get(id(sem), 0) < val:
                    e.obj.wait_ge(sem, val)
                    e.seen[id(sem)] = val


class _Stop(Exception):
    pass


def build(T, dbg=None):
    NTL = T // NT
    NB = T // 128
    nc = bass.Bass("TRN2", target_bir_lowering=False)

    def din(name, shape, dt=F32):
        return nc.dram_tensor(name, list(shape), dt, kind="ExternalInput").ap()

    xT_d = din("xT", [D, T])
    cT_d = din("cT", [128, 8])
    pos_d = din("pos", [1, T], I32)
    wada_d = din("wada", [L, D, 6 * D])
    win_d = din("win", [L, D, 2400])
    wout_d = din("wout", [L, D, D])
    ff1_d = din("ff1", [L, D, 4 * D])
    ff2_d = din("ff2", [L, 4 * D, D])
    wl_d = din("wl", [L, 128, 1152])
    wuq_d = din("wuq", [L, 256, 576])
    wk_d = din("wk", [L, 256, 384])
    wv_d = din("wv", [L, 256, 384])
    cols_d = din("cols", [L, 128, NCOL])
    const_d = din("consts", [128, NCONST])
    oT_d = nc.dram_tensor("oT", [D, T], F32, kind="ExternalOutput").ap()
    kc_d = nc.dram_tensor("kcache", [L, 6, 96, T], BF16, kind="Internal").ap()
    vc_d = nc.dram_tensor("vcache", [L, 6, 128, NB, 64], BF16, kind="Internal").ap()
    dbg_d = None
    if dbg is not None:
        dbg_d = nc.dram_tensor("dbg", [128, 16 * 512], F32, kind="ExternalOutput").ap()
    dbg_slots = {}
    R_dbg = Region("dbg")

    es = ExitStack()
    with es:
        k = KB(nc, es)

        def sb(name, shape, dt=F32):
            return es.enter_context(nc.sbuf_tensor(name, list(shape), dt))

        def ps(name):
            return es.enter_context(nc.psum_tensor(name, [128, 512], F32))

        xT = sb("xT_s", [128, 8, NT]); R_x = [Region("x%d" % i) for i in range(8)]
        hT = sb("hT_s", [128, 8, NT], BF16); R_h = Region("hT")
        mixT = sb("mixT_s", [128, 8, NT], BF16); R_mix = [Region("mix%d" % i) for i in range(8)]
        big = sb("big_s", [128, 8192]); R_big = Region("big")
        pT = big[:, 0:5120].rearrange("p (c t) -> p c t", t=NT)
        hid = big[:].bitcast(BF16).rearrange("p (c t) -> p c t", t=NT)
        wbuf = [sb("wbuf%d" % i, [128, 8192], BF16) for i in range(2)]
        R_wb = [Region("wb%d" % i) for i in range(2)]
        wl = sb("wl_s", [128, L, 1152], BF16); wuq = sb("wuq_s", [128, L, 2, 576], BF16)
        wk = sb("wk_s", [128, L, 2, 384], BF16); wv = sb("wv_s", [128, L, 2, 384], BF16)
        R_sw = Region("smallw")
        cols = sb("cols_s", [128, L, NCOL]); R_cols = Region("cols")
        dcol = sb("dcol_s", [128, L, NDC]); R_dcol = Region("dcol")
        cst = sb("cst_s", [128, NCONST]); R_cst = Region("cst")
        cstb = sb("cstb_s", [128, 256], BF16)
        cs_c = sb("csc_s", [128, 8]); cs_b = sb("csb_s", [128, 8], BF16); R_cs = Region("cs")
        hbuf = [sb("hbuf%d" % l, [128, 2, 30 + NT], BF16) for l in range(L)]
        R_hb = [Region("hb%d" % l) for l in range(L)]
        plast = sb("plast_s", [128, L, 10]); R_pl = [Region("pl%d" % l) for l in range(L)]
        Sbd = [[sb("S%d_%d" % (l, p), [128, 128]) for p in range(3)] for l in range(L)]
        R_S = [[Region("S%d_%d" % (l, p)) for p in range(3)] for l in range(L)]
        kint = sb("kint_s", [128, NT], I32)
        cosT = sb("cos_s", [128, NT]); sinT = sb("sin_s", [128, NT]); R_cos = Region("cossin")
        arena = sb("arena_s", [128, 14336]); R_ar = Region("arena")
        expd = {nm: (sb("exp_" + nm, [128, 512]), Region("exp_" + nm)) for nm in ("Abd", "Bbd", "Kbd", "Vbd")}
        banks = [ps("bank%d" % i) for i in range(8)]
        R_bk = [Region("bank%d" % i) for i in range(8)]
        R_kc = [[Region("kc%d_%d" % (l, h)) for h in range(6)] for l in range(L)]
        R_vc = [[Region("vc%d_%d" % (l, h)) for h in range(6)] for l in range(L)]
        R_out = Region("out")

        ident = cst[:, K_ID:K_ID + 128]
        ones_m = cst[:, K_ONES:K_ONES + 128]
        blk2 = cst[:, K_BLK:K_BLK + 128]
        su2 = cst[:, K_SU:K_SU + 128]
        sl2 = cst[:, K_SL:K_SL + 128]
        uis = cst[:, K_UIS:K_UIS + 64]
        rmask = cst[:, K_RM:K_RM + 512]
        swapm = cst[:, K_SWAP:K_SWAP + 128]
        rotm = cst[:, K_ROT:K_ROT + 128]
        ifr = cst[:, K_IF:K_IF + 1]

        _bk = [0]

        def bank(pool=(2, 3, 4, 5, 6)):
            _bk[0] += 1
            i = pool[_bk[0] % len(pool)]
            return banks[i], R_bk[i]

        def mm(out, lhsT, rhs, start, stop, R, W, inc=True):
            k.op("pe", lambda e: e.matmul(out, lhsT, rhs, start=start, stop=stop), R, W, inc=inc)

        def act(out, in_, func, R, W, bias=0.0, scale=1.0):
            k.op("act", lambda e: e.activation(out=out, in_=in_, func=func, bias=bias, scale=scale), R, W)

        def tt(en, out, in0, in1, op, R, W):
            k.op(en, lambda e: e.tensor_tensor(out=out, in0=in0, in1=in1, op=op), R, W)

        def ts(en, out, in0, s1, s2, op0, op1, R, W):
            if op1 is None:
                k.op(en, lambda e: e.tensor_scalar(out=out, in0=in0, scalar1=s1, scalar2=None, op0=op0), R, W)
            else:
                k.op(en, lambda e: e.tensor_scalar(out=out, in0=in0, scalar1=s1, scalar2=s2, op0=op0, op1=op1), R, W)

        def stt(en, out, in0, scalar, in1, op0, op1, R, W):
            k.op(en, lambda e: e.scalar_tensor_tensor(out=out, in0=in0, scalar=scalar, in1=in1, op0=op0, op1=op1), R, W)

        def cp(en, out, in_, R, W):
            if en == "act":
                k.op("act", lambda e: e.copy(out=out, in_=in_), R, W)
            else:
                k.op(en, lambda e: e.tensor_copy(out=out, in_=in_), R, W)

        def recip(out, in_, R, W):
            k.op("dve", lambda e: e.reciprocal(out=out, in_=in_), R, W)

        def memset(en, ap, val, W):
            k.op(en, lambda e: e.memset(ap, val), (), W)

        def chk(name, ap, R):
            if dbg is None:
                return
            names, stop = dbg
            if name in names and name not in dbg_slots:
                i = len(dbg_slots)
                dbg_slots[name] = i
                n = ap.shape[-1]
                k.dma("pool", dbg_d[0:ap.shape[0], i * 512:i * 512 + n], ap, R, [R_dbg], accumulate=True)
            if name == stop:
                raise _Stop()

        k.dma("sp", cst[:], const_d, (), [R_cst])
        k.dma("sp", cols[:], cols_d.rearrange("l p n -> p l n"), (), [R_cols])
        k.dma("sp", cs_c[:], cT_d, (), [R_cs])
        R_sw2 = [Region("sw%d" % i) for i in range(4)]
        k.dma("pool", wl[:], wl_d.rearrange("l p n -> p l n"), (), [R_sw2[0]])
        k.dma("pool", wuq[:], wuq_d.rearrange("l (c p) n -> p l c n", p=128), (), [R_sw2[1]])
        k.dma("pool", wk[:], wk_d.rearrange("l (c p) n -> p l c n", p=128), (), [R_sw2[2]])
        k.dma("pool", wv[:], wv_d.rearrange("l (c p) n -> p l c n", p=128), (), [R_sw2[3]])
        R_SW = R_sw2
        cp("dve", cstb[:, 0:128], ident, [R_cst], [R_cst])
        cp("dve", cstb[:, 128:256], cst[:, K_CM:K_CM + 128], [R_cst], [R_cst])
        identb = cstb[:, 0:128]
        cmb = cstb[:, 128:256]
        for l in range(L):
            for p in range(3):
                memset("dve", Sbd[l][p][:], 0.0, [R_S[l][p]])
            memset("dve", hbuf[l][:], 0.0, [R_hb[l]])
            memset("dve", plast[:, l, :], 0.0, [R_pl[l]])
        for nm in expd:
            memset("dve", expd[nm][0][:], 0.0, [expd[nm][1]])
        act(cs_c[:], cs_c[:], AF.Silu, [R_cs], [R_cs])
        cp("dve", cs_b[:], cs_c[:], [R_cs], [R_cs])

        _wb = [0]

        def wload(src_ap, shape_view):
            i = _wb[0] % 2
            _wb[0] += 1
            n = 1
            for s in shape_view[1:]:
                n *= s
            flat = wbuf[i][:, 0:n]
            if len(shape_view) == 3:
                view = flat.rearrange("p (a b) -> p a b", b=shape_view[2])
            else:
                view = flat
            k.dma("pool", view, src_ap, (), [R_wb[i]])
            return view, R_wb[i]

        for l in range(L):
            for g in range(8):
                wv_, Rw = wload(wada_d[l, :, g * 768:(g + 1) * 768].rearrange("(c p) n -> p c n", p=128), [128, 8, 768])
                bk, Rb = bank()
                for j in range(6):
                    for kc in range(8):
                        mm(bk[:, j:j + 1], wv_[:, kc, j * 128:(j + 1) * 128], cs_b[:, kc:kc + 1], kc == 0, kc == 7,
                           [Rw, R_cs], [Rb])
                tt("dve", dcol[:, l, MOD + g * 6:MOD + g * 6 + 6], bk[:, 0:6], cols[:, l, BADA + g * 6:BADA + g * 6 + 6],
                   ALU.add, [Rb, R_cols], [R_dcol])
            m = lambda i: dcol[:, l, MOD + i * 8:MOD + i * 8 + 8]
            stt("dve", dcol[:, l, GM1:GM1 + 8], m(1), 1.0, cols[:, l, GPM:GPM + 8], ALU.add, ALU.mult, [R_dcol, R_cols], [R_dcol])
            tt("dve", dcol[:, l, GG1:GG1 + 8], m(2), cols[:, l, GQM:GQM + 8], ALU.mult, [R_dcol, R_cols], [R_dcol])
            stt("dve", dcol[:, l, GM2:GM2 + 8], m(4), 1.0, cols[:, l, GPF:GPF + 8], ALU.add, ALU.mult, [R_dcol, R_cols], [R_dcol])
            tt("dve", dcol[:, l, GG2:GG2 + 8], m(5), cols[:, l, GQF:GQF + 8], ALU.mult, [R_dcol, R_cols], [R_dcol])
            ts("dve", dcol[:, l, OMU:OMU + 10], cols[:, l, MU:MU + 10], -1.0, 1.0, ALU.mult, ALU.add, [R_cols], [R_dcol])
            ts("dve", dcol[:, l, OMKA:OMKA + 3], cols[:, l, KA:KA + 3], -1.0, 1.0, ALU.mult, ALU.add, [R_cols], [R_dcol])

        def col(l, base, i=0):
            return cols[:, l, base + i:base + i + 1]

        def dc(l, base, i=0):
            return dcol[:, l, base + i:base + i + 1]

        def rms_rstd(out, srcs, Rsrc, nfeat, eps, sq_tmp, R_tmp, lhs=None):
            bk, Rb = bank((0, 1))
            n = len(srcs)
            for i, s in enumerate(srcs):
                st = sq_tmp[i % len(sq_tmp)]
                act(st, s, AF.Square, Rsrc, [R_tmp[i % len(sq_tmp)]])
                mm(bk[:], lhs if lhs is not None else ones_m, st, i == 0, i == n - 1, [R_tmp[i % len(sq_tmp)], R_cst], [Rb])
            act(out[0], bk[:], AF.Ln, [Rb], [out[1]], bias=eps, scale=1.0 / nfeat)
            act(out[0], out[0], AF.Exp, [out[1]], [out[1]], scale=-0.5)

        def carve(specs):
            off = 0
            d = {}
            for name, n in specs:
                d[name] = (arena[:, off:off + n], Region("ar_" + name))
                off += n
            assert off <= 14336, off
            return d

        try:
            for t in range(NTL):
                tok = slice(t * NT, (t + 1) * NT)
                for c in range(8):
                    k.dma("sp", xT[:, c, :], xT_d[c * 128:(c + 1) * 128, tok], (), [R_x[c]])
                for l in range(L):
                    k.barrier([R_ar])
                    A = carve([("t%d" % i, 512) for i in range(12)] +
                              [("BT", 512), ("KT", 512), ("VT", 512),
                               ("Am", 512), ("ATm", 512), ("Tm", 512), ("Mka", 512), ("Mbr", 256), ("Mkr", 256),
                               ("preU", 256), ("U", 256), ("la", 256), ("rstd", 512)])
                    tmp = [A["t%d" % i] for i in range(12)]
                    rstd = A["rstd"]
                    rms_rstd(rstd, [xT[:, c, :] for c in range(8)], R_x, D, 1e-6, [tmp[0][0], tmp[1][0]], [tmp[0][1], tmp[1][1]])
                    for c in range(8):
                        stt("dve", tmp[c % 2][0], xT[:, c, :], dc(l, GM1, c), rstd[0], ALU.mult, ALU.mult,
                            [R_x[c], rstd[1], R_dcol], [tmp[c % 2][1]])
                        act(hT[:, c, :], tmp[c % 2][0], AF.Identity, [tmp[c % 2][1], R_dcol], [R_h], bias=dc(l, MOD, c))
                    if t == 0 and l == 0:
                        chk('mod', dcol[:, 0, :], [R_dcol])
                        chk('h', hT[:, 0, :], [R_h])
                    for pc in range(2):
                        wv_, Rw = wload(win_d[l, :, pc * 640:(pc + 1) * 640].rearrange("(c p) n -> p c n", p=128), [128, 8, 640])
                        for j in range(5):
                            oc = pc * 5 + j
                            bk, Rb = bank((0, 1))
                            for kc in range(8):
                                mm(bk[:], wv_[:, kc, j * 128:(j + 1) * 128], hT[:, kc, :], kc == 0, kc == 7, [Rw, R_h], [Rb], inc=(kc == 7))
                            cp("act", pT[:, oc, :], bk[:], [Rb], [R_big])
                    if t == 0 and l == 0:
                        chk('p0', pT[:, 0, :], [R_big])
                    for c in range(10):
                        tm_, Rt = tmp[c % 2]
                        ts("dve", tm_[:, 0:NT - 1], pT[:, c, 0:NT - 1], col(l, MU, c), None, ALU.mult, None, [R_big, R_cols], [Rt])
                        ts("dve", tm_[:, NT - 1:NT], pT[:, c, NT - 1:NT], 1.0, None, ALU.mult, None, [R_big], [Rt])
                        act(pT[:, c, :], pT[:, c, :], AF.Identity, [R_big, R_dcol], [R_big], scale=dc(l, OMU, c))
                        tt("dve", pT[:, c, 1:NT], pT[:, c, 1:NT], tm_[:, 0:NT - 1], ALU.add, [R_big, Rt], [R_big])
                        stt("dve", pT[:, c, 0:1], plast[:, l, c:c + 1], col(l, MU, c), pT[:, c, 0:1], ALU.mult, ALU.add,
                            [R_big, R_pl[l], R_cols], [R_big])
                        cp("dve", plast[:, l, c:c + 1], tm_[:, NT - 1:NT], [Rt], [R_pl[l]])
                    zT = pT
                    if t == 0 and l == 0:
                        chk('z0', pT[:, 0, :], [R_big])
                        chk('z9', pT[:, 9, :], [R_big])
                    la = A["la"][0].bitcast(BF16)
                    R_la = A["la"][1]
                    act(la[0:32, :], zT[0:32, 9, :], AF.Tanh, [R_big], [R_la])
                    act(la[64:128, :], zT[64:128, 9, :], AF.Sigmoid, [R_big], [R_la])
                    cp("dve", la[32:64, :], zT[32:64, 9, :], [R_big], [R_la])
                    for pr in range(3):
                        rT, kT_, vT_ = zT[:, pr, :], zT[:, 3 + pr, :], zT[:, 6 + pr, :]
                        sg, cum, Pin, Pinv, Pex, a_t, g_t, kk_, kp_, bon, Rt_, tx = tmp
                        bk, Rb = bank()
                        mm(bk[:], wl[:, l, pr * 128:(pr + 1) * 128], la, True, True, [R_SW[0], R_la], [Rb])
                        act(sg[0], bk[:], AF.Sigmoid, [Rb, R_cols], [sg[1]], bias=col(l, W0, pr))
                        k.op("dve", lambda e: e.tensor_tensor_scan(out=cum[0], data0=rmask, data1=sg[0], initial=0.0,
                                                                   op0=ALU.mult, op1=ALU.add), [sg[1], R_cst], [cum[1]])
                        act(Pin[0], cum[0], AF.Exp, [cum[1]], [Pin[1]], scale=-C0)
                        act(Pinv[0], cum[0], AF.Exp, [cum[1]], [Pinv[1]], scale=C0)
                        tt("dve", sg[0], cum[0], sg[0], ALU.subtract, [cum[1], sg[1]], [sg[1]])
                        act(Pex[0], sg[0], AF.Exp, [sg[1]], [Pex[1]], scale=-C0)
                        if t == 0 and l == 0 and pr == 0:
                            chk('cum', cum[0], [cum[1]])
                            chk('Pin', Pin[0], [Pin[1]])
                            chk('Pex', Pex[0], [Pex[1]])
                        bk, Rb = bank()
                        mm(bk[:], wl[:, l, 384 + pr * 128:384 + (pr + 1) * 128], la, True, True, [R_SW[0], R_la], [Rb])
                        act(a_t[0], bk[:], AF.Sigmoid, [Rb, R_cols], [a_t[1]], bias=col(l, A0, pr))
                        bk, Rb = bank()
                        mm(bk[:], wl[:, l, 768 + pr * 128:768 + (pr + 1) * 128], la, True, True, [R_SW[0], R_la], [Rb])
                        cp("act", g_t[0], bk[:], [Rb], [g_t[1]])
                        ts("dve", kk_[0], kT_, col(l, KK, pr), None, ALU.mult, None, [R_big, R_cols], [kk_[1]])
                        act(tx[0], kk_[0], AF.Square, [kk_[1]], [tx[1]])
                        bk, Rb = bank()
                        mm(bk[:], blk2, tx[0], True, True, [tx[1], R_cst], [Rb])
                        ts("dve", tx[0], bk[:], 1e-24, None, ALU.max, None, [Rb], [tx[1]])
                        act(tx[0], tx[0], AF.Ln, [tx[1]], [tx[1]])
                        act(tx[0], tx[0], AF.Exp, [tx[1]], [tx[1]], scale=-0.5)
                        tt("dve", kk_[0], kk_[0], tx[0], ALU.mult, [kk_[1], tx[1]], [kk_[1]])
                        if t == 0 and l == 0 and pr == 0:
                            chk('a', a_t[0], [a_t[1]])
                            chk('kk', kk_[0], [kk_[1]])
                        ts("dve", kp_[0], a_t[0], col(l, KA, pr), dc(l, OMKA, pr), ALU.mult, ALU.add, [a_t[1], R_cols, R_dcol], [kp_[1]])
                        tt("dve", kp_[0], kp_[0], kT_, ALU.mult, [kp_[1], R_big], [kp_[1]])
                        stt("dve", tx[0], rT, col(l, RK, pr), kp_[0], ALU.mult, ALU.mult, [R_big, kp_[1], R_cols], [tx[1]])
                        bk, Rb = bank()
                        mm(bk[:], blk2, tx[0], True, True, [tx[1], R_cst], [Rb])
                        tt("dve", bon[0], bk[:], vT_, ALU.mult, [Rb, R_big], [bon[1]])
                        if t == 0 and l == 0 and pr == 0:
                            chk('kp', kp_[0], [kp_[1]])
                            chk('bon', bon[0], [bon[1]])
                        tt("dve", Rt_[0], rT, Pin[0], ALU.mult, [R_big, Pin[1]], [Rt_[1]])
                        stt("dve", Pex[0], kk_[0], -1.0, Pex[0], ALU.mult, ALU.mult, [kk_[1], Pex[1]], [Pex[1]])
                        tt("dve", kk_[0], kk_[0], a_t[0], ALU.mult, [kk_[1], a_t[1]], [kk_[1]])
                        tt("dve", kk_[0], kk_[0], Pinv[0], ALU.mult, [kk_[1], Pinv[1]], [kk_[1]])
                        tt("dve", kp_[0], kp_[0], Pinv[0], ALU.mult, [kp_[1], Pinv[1]], [kp_[1]])
                        ybk, Rybk = banks[7], R_bk[7]
                        for half in range(2):
                            hs = slice(half * 256, (half + 1) * 256)
                            exp_ = {}
                            for nm, src, Rs in (("Abd", Pex[0], Pex[1]), ("Bbd", kk_[0], kk_[1]), ("Kbd", kp_[0], kp_[1]), ("Vbd", vT_, R_big)):
                                dst, Rd = expd[nm]
                                d3 = dst[:].rearrange("p (c n) -> p c n", n=128)
                                s3 = src[:, hs].rearrange("p (c n) -> p c n", n=64)
                                cp("act", d3[0:64, :, 0:64], s3[0:64], [Rs], [Rd])
                                cp("dve", d3[64:128, :, 64:128], s3[64:128], [Rs], [Rd])
                                exp_[nm] = (d3, Rd)
                            for nm, src in (("BT", "Bbd"), ("KT", "Kbd"), ("VT", "Vbd")):
                                bk, Rb = bank()
                                for c in range(4):
                                    k.op("pe", lambda e, c=c, bk=bk, src=src: e.transpose(bk[:, c * 128:(c + 1) * 128], exp_[src][0][:, c, :], ident),
                                         [exp_[src][1], R_cst], [Rb])
                                cp("act", A[nm][0], bk[:], [Rb], [A[nm][1]])
                            BT3 = A["BT"][0].rearrange("p (c n) -> p c n", n=128)
                            KT3 = A["KT"][0].rearrange("p (c n) -> p c n", n=128)
                            VT3 = A["VT"][0].rearrange("p (c n) -> p c n", n=128)
                            Abd3, Bbd3, Kbd3 = exp_["Abd"][0], exp_["Bbd"][0], exp_["Kbd"][0]
                            R_Abd, R_Bbd, R_Kbd = exp_["Abd"][1], exp_["Bbd"][1], exp_["Kbd"][1]
                            su4 = su2.unsqueeze(1).to_broadcast([128, 4, 128])
                            sl4 = sl2.unsqueeze(1).to_broadcast([128, 4, 128])
                            ui4 = uis.unsqueeze(1).to_broadcast([128, 4, 64])

                            def gram(dst, lhs3, Rl, rhs3, Rr, mask4, n):
                                bk, Rb = bank()
                                for c in range(4):
                                    mm(bk[:, c * n:(c + 1) * n], lhs3[:, c, :], rhs3[:, c, :], True, True, [Rl, Rr], [Rb])
                                tt("dve", dst[0].rearrange("p (c n) -> p c n", n=n), bk[:, 0:4 * n].rearrange("p (c n) -> p c n", n=n),
                                   mask4, ALU.mult, [Rb, R_cst], [dst[1]])

                            gram(A["Am"], Bbd3, R_Bbd, Abd3, R_Abd, su4, 128)
                            gram(A["ATm"], Abd3, R_Abd, Bbd3, R_Bbd, sl4, 128)
                            gram(A["Mka"], Kbd3, R_Kbd, Abd3, R_Abd, su4, 128)
                            Rt3 = Rt_[0][:, hs].rearrange("p (c n) -> p c n", n=64)
                            gram(A["Mbr"], Bbd3, R_Bbd, Rt3, Rt_[1], ui4, 64)
                            gram(A["Mkr"], Kbd3, R_Kbd, Rt3, Rt_[1], ui4, 64)
                            Am3 = A["Am"][0].rearrange("p (c n) -> p c n", n=128)
                            ATm3 = A["ATm"][0].rearrange("p (c n) -> p c n", n=128)
                            Tm3 = A["Tm"][0].rearrange("p (c n) -> p c n", n=128)
                            Mka3 = A["Mka"][0].rearrange("p (c n) -> p c n", n=128)
                            Mbr3 = A["Mbr"][0].rearrange("p (c n) -> p c n", n=64)
                            Mkr3 = A["Mkr"][0].rearrange("p (c n) -> p c n", n=64)
                            R_Am, R_ATm, R_Tm = A["Am"][1], A["ATm"][1], A["Tm"][1]
                            tt("dve", Tm3, Am3, ident.unsqueeze(1).to_broadcast([128, 4, 128]), ALU.add, [R_Am, R_cst], [R_Tm])
                            for j in range(1, 6):
                                bkA, RbA = bank()
                                bkT, RbT = bank()
                                if j < 5:
                                    for c in range(4):
                                        mm(bkA[:, c * 128:(c + 1) * 128], ATm3[:, c, :], Am3[:, c, :], True, True, [R_Am, R_ATm], [RbA])
                                for c in range(4):
                                    mm(bkT[:, c * 128:(c + 1) * 128], Am3[:, c, :], ATm3[:, c, :], True, True, [R_Am, R_ATm], [RbT])
                                if j < 5:
                                    cp("act", A["Am"][0], bkA[:], [RbA], [R_Am])
                                cp("dve", A["ATm"][0], bkT[:], [RbT], [R_ATm])
                                bkU, RbU = bank()
                                for c in range(4):
                                    mm(bkU[:, c * 128:(c + 1) * 128], ATm3[:, c, :], Tm3[:, c, :], True, True, [R_ATm, R_Tm], [RbU])
                                tt("dve", A["Tm"][0], A["Tm"][0], bkU[:], ALU.add, [R_Tm, RbU], [R_Tm])
                            S_, RS_ = Sbd[l][pr], R_S[l][pr]
                            preU, U_ = A["preU"], A["U"]
                            for c in range(4):
                                cg = half * 4 + c
                                bk, Rb = bank()
                                mm(bk[:, 0:128], Abd3[:, c, :], S_[:], True, False, [R_Abd, RS_], [Rb])
                                mm(bk[:, 0:128], Mka3[:, c, :], VT3[:, c, :], False, True, [A["Mka"][1], A["VT"][1]], [Rb])
                                cp("act", preU[0][:, 0:128], bk[:, 0:128], [Rb], [preU[1]])
                                bk2, Rb2 = bank()
                                mm(bk2[:, 0:128], Tm3[:, c, :], preU[0][:, 0:128], True, True, [R_Tm, preU[1]], [Rb2])
                                cp("dve", U_[0][:, 0:128], bk2[:, 0:128], [Rb2], [U_[1]])
                                yc = ybk[:, cg * 64:(cg + 1) * 64]
                                mm(yc, S_[:], Rt_[0][:, cg * 64:(cg + 1) * 64], True, False, [RS_, Rt_[1]], [Rybk])
                                mm(yc, U_[0][:, 0:128], Mbr3[:, c, :], False, False, [U_[1], A["Mbr"][1]], [Rybk])
                                mm(yc, VT3[:, c, :], Mkr3[:, c, :], False, True, [A["VT"][1], A["Mkr"][1]], [Rybk])
                                bk3, Rb3 = bank()
                                mm(bk3[:, 0:128], ident, S_[:], True, False, [R_cst, RS_], [Rb3])
                                mm(bk3[:, 0:128], BT3[:, c, :], U_[0][:, 0:128], False, False, [A["BT"][1], U_[1]], [Rb3])
                                mm(bk3[:, 0:128], KT3[:, c, :], VT3[:, c, :], False, True, [A["KT"][1], A["VT"][1]], [Rb3])
                                ts("dve", S_[:], bk3[:, 0:128], Pin[0][:, cg * 64 + 63:cg * 64 + 64], None, ALU.mult, None,
                                   [Rb3, Pin[1]], [RS_])
                        ysb = sg
                        cp("act", ysb[0], ybk[:], [Rybk], [ysb[1]])
                        if t == 0 and l == 0 and pr == 0:
                            chk('y', ysb[0], [ysb[1]])
                        bk, Rb = bank()
                        mm(bk[:], blk2, ysb[0], True, True, [ysb[1], R_cst], [Rb])
                        stt("dve", ysb[0], bk[:], -1.0 / 64, ysb[0], ALU.mult, ALU.add, [Rb, ysb[1]], [ysb[1]])
                        act(tx[0], ysb[0], AF.Square, [ysb[1]], [tx[1]])
                        bk, Rb = bank()
                        mm(bk[:], blk2, tx[0], True, True, [tx[1], R_cst], [Rb])
                        act(tx[0], bk[:], AF.Ln, [Rb], [tx[1]], bias=64e-5, scale=1.0 / 64)
                        act(tx[0], tx[0], AF.Exp, [tx[1]], [tx[1]], scale=-0.5)
                        tt("dve", ysb[0], ysb[0], tx[0], ALU.mult, [ysb[1], tx[1]], [ysb[1]])
                        ts("dve", ysb[0], ysb[0], col(l, LNW, pr), col(l, LNB, pr), ALU.mult, ALU.add, [ysb[1], R_cols], [ysb[1]])
                        tt("dve", ysb[0], ysb[0], bon[0], ALU.add, [ysb[1], bon[1]], [ysb[1]])
                        tt("dve", mixT[:, pr, :], ysb[0], g_t[0], ALU.mult, [ysb[1], g_t[1]], [R_mix[pr]])
                        if t == 0 and l == 0 and pr == 0:
                            chk('ya', mixT[:, 0, :], [R_mix[0]])

                    k.barrier([R_ar])
                    Bm = carve([("dg", 512)] + [("f%d" % i, 512) for i in range(5)] + [("acc0", 512), ("acc1", 512), ("rq", 512), ("rkv", 512),
                               ("qn", 512), ("kvn", 512), ("qT", 1536), ("kTn", 1536), ("vnew", 768),
                               ("kb0", 512), ("kb1", 512), ("vb0", 512), ("vb1", 512), ("pt0", 256), ("pt1", 256), ("pt2", 256),
                               ("osb", 512), ("rde", 512), ("rdo", 512)])
                    ft = [Bm["f%d" % i] for i in range(5)]
                    for pc, (c0_, ncol, nch) in enumerate(((1280, 640, 5), (1920, 480, 4))):
                        wv_, Rw = wload(win_d[l, :, c0_:c0_ + ncol].rearrange("(c p) n -> p c n", p=128), [128, 8, ncol])
                        for j in range(nch):
                            oc = pc * 5 + j
                            M = 96 if oc == 8 else 128
                            bk, Rb = bank((0, 1))
                            for kc in range(8):
                                mm(bk[0:M, :], wv_[:, kc, j * 128:j * 128 + M], hT[:, kc, :], kc == 0, kc == 7, [Rw, R_h], [Rb], inc=(kc == 7))
                            cp("act", pT[0:M, oc, :], bk[0:M, :], [Rb], [R_big])
                    hb, Rhb = hbuf[l], R_hb[l]
                    for ch in range(2):
                        act(ft[0][0], pT[:, 2 + ch, :], AF.Sigmoid, [R_big], [ft[0][1]])
                        tt("dve", hb[:, ch, 30:30 + NT], pT[:, ch, :], ft[0][0], ALU.mult, [R_big, ft[0][1]], [Rhb])
                    accs = [Bm["acc0"], Bm["acc1"]]
                    dgv = Bm["dg"][0].bitcast(BF16).rearrange("p (i n) -> p i n", n=128)
                    R_dg = [Region("dg%d" % i) for i in range(8)]
                    di = 0
                    for ch in range(2):
                        acc, Racc = accs[ch]
                        bk, Rb = bank((0, 1))
                        for j in range(31):
                            sl = di % 8
                            di += 1
                            if sl % 2 == 0:
                                ts("dve", dgv[:, sl, :], identb, col(l, CW, j * 2 + ch), None, ALU.mult, None, [R_cst, R_cols], [R_dg[sl]])
                            else:
                                act(dgv[:, sl, :], identb, AF.Identity, [R_cst, R_cols], [R_dg[sl]], scale=col(l, CW, j * 2 + ch))
                            mm(bk[:], dgv[:, sl, :], hb[:, ch, j:j + NT], j == 0, j == 30, [R_dg[sl], Rhb], [Rb])
                        act(acc, bk[:], AF.Identity, [Rb, R_cols], [Racc], bias=col(l, CB, ch))
                    for ch in range(2):
                        cp("act", hb[:, ch, 0:30], hb[:, ch, NT:NT + 30], [Rhb] + [a[1] for a in accs], [Rhb])
                    bk, Rb = bank()
                    for ch in range(2):
                        mm(bk[:], ones_m, accs[ch][0], ch == 0, ch == 1, [accs[ch][1], R_cst], [Rb])
                    for ch in range(2):
                        stt("dve", accs[ch][0], bk[:], -1.0 / 256, accs[ch][0], ALU.mult, ALU.add, [Rb, accs[ch][1]], [accs[ch][1]])
                    rms_rstd(ft[2], [accs[0][0], accs[1][0]], [accs[0][1], accs[1][1]], 256, 1e-5, [ft[0][0], ft[1][0]], [ft[0][1], ft[1][1]])
                    for ch in range(2):
                        tt("dve", accs[ch][0], accs[ch][0], ft[2][0], ALU.mult, [accs[ch][1], ft[2][1]], [accs[ch][1]])
                        ts("dve", accs[ch][0], accs[ch][0], col(l, CLW, ch), col(l, CLB, ch), ALU.mult, ALU.add, [accs[ch][1], R_cols], [accs[ch][1]])
                        act(mixT[:, 3 + ch, :], accs[ch][0], AF.Silu, [accs[ch][1]], [R_mix[3 + ch]])
                    if t == 0 and l == 0:
                        chk('yb', mixT[:, 3, :], [R_mix[3]])
                    if l == 0:
                        k.dma("pool", ft[4][0][64:96, :], pos_d[:, tok].partition_broadcast(32).rearrange("p o t -> p (o t)"), (), [ft[4][1]])
                        ang = ft[4][0][64:96, :]
                        ts("dve", ang, ang, ifr[64:96, :], None, ALU.mult, None, [ft[4][1], R_cst], [ft[4][1]])
                        chk('ang0', ang, [ft[4][1]])
                        kf = ft[3][0][64:96, :]
                        ki = kint[64:96, :]
                        ts("dve", kf, ang, float(1.0 / (2 * np.pi)), None, ALU.mult, None, [ft[4][1]], [ft[3][1]])
                        cp("dve", ki, kf, [ft[3][1]], [ft[2][1]])
                        cp("dve", kf, ki, [ft[2][1]], [ft[3][1]])
                        chk('kf', kf, [ft[3][1]])
                        for cc in (6.28125, 1.9350051879882812e-03, 3.0199159819567e-07):
                            stt("dve", ang, kf, -cc, ang, ALU.mult, ALU.add, [ft[3][1], ft[4][1]], [ft[4][1]])
                        ts("dve", kf, ang, float(np.pi), float(-2 * np.pi), ALU.is_gt, ALU.mult, [ft[4][1]], [ft[3][1]])
                        tt("dve", ang, ang, kf, ALU.add, [ft[4][1], ft[3][1]], [ft[4][1]])
                        ts("dve", kf, ang, float(-np.pi), float(2 * np.pi), ALU.is_lt, ALU.mult, [ft[4][1]], [ft[3][1]])
                        tt("dve", ang, ang, kf, ALU.add, [ft[4][1], ft[3][1]], [ft[4][1]])
                        chk('red', ang, [ft[4][1]])
                        act(sinT[64:96, :], ang, AF.Sin, [ft[4][1]], [R_cos])
                        act(kf, ang, AF.Sin, [ft[4][1]], [ft[3][1]], scale=0.5)
                        tt("dve", kf, kf, kf, ALU.mult, [ft[3][1]], [ft[3][1]])
                        ts("dve", cosT[64:96, :], kf, -2.0, 1.0, ALU.mult, ALU.add, [ft[3][1]], [R_cos])
                        if t == 0:
                            chk('cos', cosT[64:96, :], [R_cos])
                            chk('sin', sinT[64:96, :], [R_cos])
                    rq, rkv = Bm["rq"], Bm["rkv"]
                    rms_rstd(rq, [pT[:, 4, :], pT[:, 5, :]], [R_big], 256, 1e-6, [ft[0][0], ft[1][0]], [ft[0][1], ft[1][1]])
                    rms_rstd(rkv, [pT[:, 6, :], pT[:, 7, :]], [R_big], 256, 1e-6, [ft[0][0], ft[1][0]], [ft[0][1], ft[1][1]])
                    qn = Bm["qn"][0].bitcast(BF16).rearrange("p (c t) -> p c t", t=NT)
                    kvn = Bm["kvn"][0].bitcast(BF16).rearrange("p (c t) -> p c t", t=NT)
                    for c in range(2):
                        stt("dve", qn[:, c, :], pT[:, 4 + c, :], col(l, QN, c), rq[0], ALU.mult, ALU.mult, [R_big, rq[1], R_cols], [Bm["qn"][1]])
                        stt("dve", kvn[:, c, :], pT[:, 6 + c, :], col(l, KVN, c), rkv[0], ALU.mult, ALU.mult, [R_big, rkv[1], R_cols], [Bm["kvn"][1]])
                    qT = Bm["qT"][0].bitcast(BF16).rearrange("p (h t) -> p h t", t=NT)
                    kTn = Bm["kTn"][0].bitcast(BF16).rearrange("p (h t) -> p h t", t=NT)
                    vnew = Bm["vnew"][0].bitcast(BF16).rearrange("p (b n) -> p b n", n=384)
                    R_qT, R_kTn, R_vnew = Bm["qT"][1], Bm["kTn"][1], Bm["vnew"][1]

                    def rope(dst, raw, Rraw, Rdst):
                        bk, Rb = bank()
                        mm(bk[0:96, :], rotm[0:96, 0:96], raw[0:96, :], True, True, [Rraw, R_cst], [Rb])
                        tt("dve", ft[1][0][64:96, :], bk[64:96, :], sinT[64:96, :], ALU.mult, [Rb, R_cos], [ft[1][1]])
                        tt("dve", ft[2][0][64:96, :], raw[64:96, :], cosT[64:96, :], ALU.mult, [Rraw, R_cos], [ft[2][1]])
                        tt("dve", dst, ft[1][0][64:96, :], ft[2][0][64:96, :], ALU.add, [ft[1][1], ft[2][1]], [Rdst])

                    kpe, Rkpe = ft[3]
                    rope(kpe.bitcast(BF16)[64:96, 0:NT], pT[:, 8, :], R_big, Rkpe)
                    for h in range(6):
                        bk, Rb = bank()
                        for c in range(2):
                            mm(bk[0:96, :], wuq[:, l, c, h * 96:(h + 1) * 96], qn[:, c, :], c == 0, c == 1, [R_SW[1], Bm["qn"][1]], [Rb], inc=(c == 1))
                        cp("act", ft[0][0][0:96, :], bk[0:96, :], [Rb], [ft[0][1]])
                        cp("act", qT[0:64, h, :], bk[0:64, :], [Rb], [R_qT])
                        rope(qT[64:96, h, :], ft[0][0], ft[0][1], R_qT)
                        bk, Rb = bank()
                        for c in range(2):
                            mm(bk[0:64, :], wk[:, l, c, h * 64:(h + 1) * 64], kvn[:, c, :], c == 0, c == 1, [R_SW[2], Bm["kvn"][1]], [Rb], inc=(c == 1))
                        cp("act", kTn[0:64, h, :], bk[0:64, :], [Rb], [R_kTn])
                        cp("dve", kTn[64:96, h, :], kpe.bitcast(BF16)[64:96, 0:NT], [Rkpe], [R_kTn])
                    for b4 in range(4):
                        bk, Rb = bank()
                        for c in range(2):
                            mm(bk[:, 0:384], kvn[:, c, b4 * 128:(b4 + 1) * 128], wv[:, l, c, :], c == 0, c == 1, [R_SW[3], Bm["kvn"][1]], [Rb], inc=(c == 1))
                        cp("act", vnew[:, b4, :], bk[:, 0:384], [Rb], [R_vnew])
                    if t == 0 and l == 0:
                        chk('q0', qT[0:96, 0, :], [R_qT])
                        chk('k0', kTn[0:96, 0, :], [R_kTn])
                        chk('v0', vnew[:, 0, :], [R_vnew])
                    for h in range(6):
                        k.dma("sp", kc_d[l, h, :, tok], kTn[0:96, h, :], [R_kTn], [R_kc[l][h]])
                        k.dma("sp", vc_d[l, h, :, t * 4:(t + 1) * 4, :], vnew[:, :, h * 64:(h + 1) * 64], [R_vnew], [R_vc[l][h]])
                    kbs = [Bm["kb0"], Bm["kb1"]]
                    vbs = [Bm["vb0"], Bm["vb1"]]
                    pts = [Bm["pt0"], Bm["pt1"], Bm["pt2"]]
                    osb = Bm["osb"]
                    rds = [Bm["rde"], Bm["rdo"]]
                    for i in range(2):
                        memset("dve", vbs[i][0].bitcast(BF16), 1.0, [vbs[i][1]])
                        memset("dve", rds[i][0], 0.0, [rds[i][1]])
                    nseg = t + 1
                    scale = float(96 ** -0.5)
                    blocks = [(h, s_, jb) for h in range(6) for s_ in range(nseg) for jb in range(4)]
                    segs = {}
                    qk = {}
                    li = [0]

                    def emit_qk(i):
                        h, s_, jb = blocks[i]
                        par = h % 2
                        if (h, s_) not in segs:
                            kb, Rkb = kbs[li[0] % 2]
                            vb, Rvb = vbs[li[0] % 2]
                            li[0] += 1
                            kbv = kb.bitcast(BF16)[0:96, 0:NT]
                            vbv = vb.bitcast(BF16).rearrange("p (b n) -> p b n", n=128)
                            k.dma("sp", kbv, kc_d[l, h, :, s_ * NT:(s_ + 1) * NT], [R_kc[l][h]], [Rkb])
                            if par == 0:
                                k.dma("sp", vbv[:, 0:4, 0:64], vc_d[l, h, :, s_ * 4:(s_ + 1) * 4, :], [R_vc[l][h]], [Rvb])
                            else:
                                k.dma("sp", vbv[:, 4:8, 64:128], vc_d[l, h, :, s_ * 4:(s_ + 1) * 4, :], [R_vc[l][h]], [Rvb])
                            segs[(h, s_)] = (kbv, Rkb, vbv, Rvb)
                        kbv, Rkb, vbv, Rvb = segs[(h, s_)]
                        diag = (s_ == t)
                        q0 = jb * 128 if diag else 0
                        nq = NT - q0
                        sbk, Rsb = bank()
                        mm(sbk[:, 0:nq], kbv[:, jb * 128:(jb + 1) * 128], qT[0:96, h, q0:NT], True, True, [Rkb, R_qT], [Rsb])
                        qk[i] = (sbk, Rsb, q0, nq, diag)

                    LA = 2
                    nxt = 0
                    for i in range(len(blocks)):
                        while nxt <= min(i + LA, len(blocks) - 1):
                            emit_qk(nxt)
                            nxt += 1
                        h, s_, jb = blocks[i]
                        par = h % 2
                        sbk, Rsb, q0, nq, diag = qk.pop(i)
                        kbv, Rkb, vbv, Rvb = segs[(h, s_)]
                        obk, Robk = banks[h % 2], R_bk[h % 2]
                        pt, Rpt = pts[i % 3]
                        ptv = pt.bitcast(BF16)[:, 0:NT]
                        act(ptv[:, 0:nq], sbk[:, 0:nq], AF.Exp, [Rsb], [Rpt], scale=scale)
                        if diag:
                            tt("dve", ptv[:, 0:128], ptv[:, 0:128], cmb, ALU.mult, [Rpt, R_cst], [Rpt])
                        first = (s_ == 0 and jb == 0)
                        last = (s_ == nseg - 1) and (jb == 3)
                        vsl = vbv[:, jb, :] if par == 0 else vbv[:, 4 + jb, :]
                        mm(obk[:, q0:NT], vsl, ptv[:, 0:nq], first, last, [Rvb, Rpt], [Robk], inc=True)
                        if last:
                            olo, dlo = (0, 64) if par == 0 else (64, 0)
                            cp("act", osb[0][olo:olo + 64, :], obk[olo:olo + 64, :], [Robk], [osb[1]])
                            rd, Rrd = rds[par]
                            recip(rd[dlo:dlo + 64, :], obk[dlo:dlo + 64, :], [Robk], [Rrd])
                            bk, Rb = bank()
                            mm(bk[:], swapm, rd, True, True, [Rrd, R_cst], [Rb])
                            tt("dve", mixT[olo:olo + 64, 5 + h // 2, :], osb[0][olo:olo + 64, :], bk[olo:olo + 64, :], ALU.mult,
                               [osb[1], Rb], [R_mix[5 + h // 2]])
                    if t == 0 and l == 0:
                        chk('yc', mixT[:, 5, :], [R_mix[5]])

                    k.barrier([R_ar])
                    Cm = carve([("y", 4096), ("rs", 512), ("s0", 512), ("s1", 512), ("u0", 512), ("u1", 512)])
                    yT = Cm["y"][0].rearrange("p (c t) -> p c t", t=NT)
                    R_y = Cm["y"][1]

                    def resid(gbase):
                        rms_rstd(Cm["rs"], [yT[:, c, :] for c in range(8)], [R_y], D, 1e-6, [Cm["s0"][0], Cm["s1"][0]], [Cm["s0"][1], Cm["s1"][1]])
                        for c in range(8):
                            u, Ru = Cm["u%d" % (c % 2)]
                            stt("dve", u, yT[:, c, :], dc(l, gbase, c), Cm["rs"][0], ALU.mult, ALU.mult, [R_y, Cm["rs"][1], R_dcol], [Ru])
                            tt("dve", xT[:, c, :], xT[:, c, :], u, ALU.add, [R_x[c], Ru], [R_x[c]])

                    wv_, Rw = wload(wout_d[l].rearrange("(c p) n -> p c n", p=128), [128, 8, 1024])
                    for oc in range(8):
                        bk, Rb = bank((0, 1))
                        for kc in range(8):
                            mm(bk[:], wv_[:, kc, oc * 128:(oc + 1) * 128], mixT[:, kc, :], kc == 0, kc == 7, [Rw, R_mix[kc]], [Rb], inc=(kc == 7))
                        cp("act", yT[:, oc, :], bk[:], [Rb], [R_y])
                    resid(GG1)
                    if t == 0 and l == 0:
                        chk('x1', xT[:, 0, :], [R_x[0]])
                    rms_rstd(Cm["rs"], [xT[:, c, :] for c in range(8)], R_x, D, 1e-6, [Cm["s0"][0], Cm["s1"][0]], [Cm["s0"][1], Cm["s1"][1]])
                    for c in range(8):
                        u, Ru = Cm["u%d" % (c % 2)]
                        stt("dve", u, xT[:, c, :], dc(l, GM2, c), Cm["rs"][0], ALU.mult, ALU.mult, [R_x[c], Cm["rs"][1], R_dcol], [Ru])
                        act(hT[:, c, :], u, AF.Identity, [Ru, R_dcol], [R_h], bias=dc(l, MOD + 24, c))
                    for pc in range(4):
                        wv_, Rw = wload(ff1_d[l, :, pc * 1024:(pc + 1) * 1024].rearrange("(c p) n -> p c n", p=128), [128, 8, 1024])
                        for j in range(8):
                            hc = pc * 8 + j
                            bk, Rb = bank((0, 1))
                            for kc in range(8):
                                mm(bk[:], wv_[:, kc, j * 128:(j + 1) * 128], hT[:, kc, :], kc == 0, kc == 7, [Rw, R_h], [Rb], inc=(kc == 7))
                            s_, Rs_ = Cm["s%d" % (hc % 2)]
                            act(s_, bk[:], AF.Square, [Rb], [Rs_])
                            stt("dve", hid[:, hc, :], bk[:], 0.0, s_, ALU.is_gt, ALU.mult, [Rb, Rs_], [R_big])
                    for pc in range(4):
                        wv_, Rw = wload(ff2_d[l, :, pc * 256:(pc + 1) * 256].rearrange("(c p) n -> p c n", p=128), [128, 32, 256])
                        for j in range(2):
                            oc = pc * 2 + j
                            bk, Rb = bank((0, 1))
                            for kc in range(32):
                                mm(bk[:], wv_[:, kc, j * 128:(j + 1) * 128], hid[:, kc, :], kc == 0, kc == 31, [Rw, R_big], [Rb], inc=(kc == 31))
                            cp("act", yT[:, oc, :], bk[:], [Rb], [R_y])
                    resid(GG2)
                    if t == 0 and l == 0:
                        chk('x2', xT[:, 0, :], [R_x[0]])
                for c in range(8):
                    k.dma("sp", oT_d[c * 128:(c + 1) * 128, tok], xT[:, c, :], [R_x[c]], [R_out], accumulate=True)
        except _Stop:
            pass
        k.finish([R_out, R_dbg])
        print("instructions emitted:", k.nins, "sems:", k.nsem)
    _DBG_SLOTS.clear()
    _DBG_SLOTS.update(dbg_slots)
    return nc


def _consts():
    c = np.zeros((128, NCONST), np.float32)
    c[:, K_ID:K_ID + 128] = np.eye(128)
    c[:, K_ONES:K_ONES + 128] = 1.0
    i = np.arange(128)
    same = (i[:, None] // 64) == (i[None, :] // 64)
    c[:, K_BLK:K_BLK + 128] = same
    li = i % 64
    c[:, K_SU:K_SU + 128] = same & (li[:, None] < li[None, :])
    c[:, K_SL:K_SL + 128] = same & (li[:, None] > li[None, :])
    c[:, K_UIS:K_UIS + 64] = li[:, None] <= np.arange(64)[None, :]
    c[:, K_CM:K_CM + 128] = i[:, None] <= i[None, :]
    rm = np.ones(512, np.float32)
    rm[::64] = 0.0
    c[:, K_RM:K_RM + 512] = rm[None, :]
    sw = np.zeros((128, 128), np.float32)
    sw[(i + 64) % 128, i] = 1.0
    c[:, K_SWAP:K_SWAP + 128] = sw
    rot = np.zeros((128, 128), np.float32)
    for r in range(16):
        rot[64 + r + 16, 64 + r] = -1.0
        rot[64 + r, 64 + r + 16] = 1.0
    c[:, K_ROT:K_ROT + 128] = rot
    inv = (1.0 / (np.float32(10000.0) ** (np.arange(0, 32, 2, dtype=np.float32) / np.float32(32)))).astype(np.float32)
    c[64:96, K_IF] = np.concatenate([inv, inv])
    return c


def _colize(v):
    v = np.asarray(v, np.float32).reshape(-1, 128)
    return np.ascontiguousarray(v.T)


def _prep(inp, T):
    f = lambda a: np.ascontiguousarray(np.asarray(a, np.float32))
    cols = np.zeros((L, 128, NCOL), np.float32)
    wl = np.zeros((L, 128, 1152), np.float32)
    for l in range(L):
        def put(base, v):
            cc = _colize(v)
            cols[l, :, base:base + cc.shape[1]] = cc
        put(GPM, inp["g_pre_mix"][l]); put(GQM, inp["g_post_mix"][l]); put(GPF, inp["g_pre_ffn"][l]); put(GQF, inp["g_post_ffn"][l])
        put(BADA, inp["b_ada"][l])
        put(MU, inp["rwkv_mu"][l]); put(W0, inp["rwkv_w0"][l]); put(A0, inp["rwkv_a0"][l]); put(KK, inp["rwkv_k_k"][l])
        put(KA, inp["rwkv_k_a"][l]); put(RK, np.asarray(inp["rwkv_r_k"][l]).reshape(-1)); put(LNW, inp["rwkv_ln_w"][l])
        put(LNB, inp["rwkv_ln_b"][l]); put(CB, inp["conv_b"][l]); put(CLW, inp["conv_ln_w"][l]); put(CLB, inp["conv_ln_b"][l])
        put(QN, inp["mla_q_norm"][l]); put(KVN, inp["mla_kv_norm"][l])
        cw = np.asarray(inp["conv_w"][l], np.float32)
        for j in range(31):
            for ch in range(2):
                cols[l, :, CW + j * 2 + ch] = cw[j, ch * 128:(ch + 1) * 128]
        wl[l, 0:32, 0:384] = inp["rwkv_w2"][l]
        wl[l, 32:64, 384:768] = inp["rwkv_a2"][l]
        wl[l, 64:128, 768:1152] = inp["rwkv_g2"][l]
    w_in = np.asarray(inp["w_in"], np.float32)
    win = np.concatenate([w_in[:, :, 0:2304], w_in[:, :, 2240:2304], w_in[:, :, 2304:2336]], axis=2)
    wukv = np.asarray(inp["mla_w_ukv"], np.float32).reshape(L, 256, 6, 128)
    wk = np.ascontiguousarray(wukv[:, :, :, 0:64].reshape(L, 256, 384))
    wv = np.ascontiguousarray(wukv[:, :, :, 64:128].reshape(L, 256, 384))
    shared = {
        "wada": f(inp["w_ada"]), "win": f(win), "wout": f(inp["w_out"]), "ff1": f(inp["w_ff1"]), "ff2": f(inp["w_ff2"]),
        "wl": wl, "wuq": f(inp["mla_w_uq"]), "wk": wk, "wv": wv, "cols": cols, "consts": _consts(),
    }
    maps = []
    x = np.asarray(inp["x"], np.float32)
    c = np.asarray(inp["c"], np.float32)
    pos = np.asarray(inp["positions"], np.int32)
    nb = x.shape[0]
    for core in range(8):
        b = core % nb
        m = dict(shared)
        m["xT"] = np.ascontiguousarray(x[b, :T].T)
        m["cT"] = _colize(c[b])
        m["pos"] = np.ascontiguousarray(pos[b:b + 1, :T])
        maps.append(m)
    return maps


_NC_CACHE = {}
_DBG_SLOTS = {}


def run_dbg(inp, T, names, stop):
    nc = build(T, dbg=(names, stop))
    maps = _prep(inp, T)
    res = run_bass_kernel_spmd(nc, maps, core_ids=list(range(8)))
    return res.results[0]["dbg"], dict(_DBG_SLOTS)


def run(inp, T):
    if T not in _NC_CACHE:
        _NC_CACHE[T] = build(T)
    nc = _NC_CACHE[T]
    maps = _prep(inp, T)
    res = run_bass_kernel_spmd(nc, maps, core_ids=list(range(8)))
    nb = np.asarray(inp["x"]).shape[0]
    out = np.stack([np.ascontiguousarray(res.results[b]["oT"].T) for b in range(nb)], axis=0)
    return out.astype(np.float32)


def kernel(**inputs):
    T = np.asarray(inputs["x"]).shape[1]
    return run(inputs, T)
```

```python
import os
import numpy as np
from contextlib import ExitStack
import concourse.bass as bass
import concourse.mybir as mybir
from concourse.bass_utils import run_bass_kernel_spmd

F32 = mybir.dt.float32
BF16 = mybir.dt.bfloat16
I32 = mybir.dt.int32
AF = mybir.ActivationFunctionType
ALU = mybir.AluOpType

D = 1024
NT = 512
L = 2
C0 = float(np.exp(-0.5))
GPM, GQM, GPF, GQF, BADA, MU, W0, A0, KK, KA, RK, LNW, LNB, CB, CLW, CLB, QN, KVN, CW = (
    0, 8, 16, 24, 32, 80, 90, 93, 96, 99, 102, 105, 108, 111, 113, 115, 117, 119, 121)
NCOL = 121 + 62
OMU, OMKA, MOD, GM1, GG1, GM2, GG2, NDC = 0, 10, 13, 61, 69, 77, 85, 93
K_ID, K_ONES, K_BLK, K_SU, K_SL, K_UIS, K_CM, K_RM, K_SWAP, K_ROT, K_IF, NCONST = (
    0, 128, 256, 384, 512, 640, 704, 832, 1344, 1472, 1600, 1601)


class Region:
    __slots__ = ("name", "w", "r", "dsem", "dcnt")

    def __init__(self, name):
        self.name = name
        self.w = None
        self.r = {}
        self.dsem = None
        self.dcnt = 0


class Eng:
    def __init__(self, name, obj, sem):
        self.name = name
        self.obj = obj
        self.sem = sem
        self.cnt = 0
        self.seen = {}


class KB:
    def __init__(self, nc, es):
        self.nc = nc
        self.es = es
        self.nsem = 0
        self.E = {}
        for n, o in (("pe", nc.tensor), ("act", nc.scalar), ("dve", nc.vector), ("pool", nc.gpsimd), ("sp", nc.sync)):
            self.E[n] = Eng(n, o, self.newsem("e_" + n))
        self.nins = 0
        self.dsems = {}

    def newsem(self, name):
        self.nsem += 1
        return self.es.enter_context(self.nc.semaphore(name + "_%d" % self.nsem))

    def _waits(self, e, reads, writes, skip_dma_waw=None):
        need = {}

        def add(ev):
            if ev is None:
                return
            sem, val = ev
            if sem is e.sem:
                return
            if e.seen.get(id(sem), 0) >= val:
                return
            if need.get(id(sem), (None, 0))[1] < val:
                need[id(sem)] = (sem, val)

        for R in reads:
            add(R.w)
        for R in writes:
            if not (skip_dma_waw is not None and R.w is not None and R.w[0] is skip_dma_waw):
                add(R.w)
            for ev in R.r.values():
                add(ev)
        for sem, val in need.values():
            e.obj.wait_ge(sem, val)
            e.seen[id(sem)] = val

    def op(self, en, fn, reads=(), writes=(), inc=True):
        e = self.E[en]
        self._waits(e, reads, writes)
        ins = fn(e.obj)
        if inc:
            e.cnt += 1
            ins.then_inc(e.sem, 1)
            ev = (e.sem, e.cnt)
        else:
            ev = (e.sem, e.cnt + 1)
        for R in reads:
            R.r[id(e.sem)] = ev
        for R in writes:
            R.w = ev
            R.r = {}
        self.nins += 1
        return ins

    def dma(self, qn, out, in_, reads=(), writes=(), accumulate=False):
        q = self.E[qn]
        W = writes[0]
        if W.name not in self.dsems:
            self.dsems[W.name] = [self.newsem("d_" + W.name), 0]
        ds = self.dsems[W.name]
        W.dsem = ds[0]
        self._waits(q, reads, writes, skip_dma_waw=W.dsem if accumulate else None)
        ins = q.obj.dma_start(out=out, in_=in_)
        ds[1] += 16
        ins.then_inc(W.dsem, 16)
        ev = (W.dsem, ds[1])
        for R in reads:
            R.r[id(W.dsem)] = ev
        for R in writes:
            R.w = ev
            R.r = {}
        self.nins += 1

    def barrier(self, regions=()):
        ce = ["pe", "act", "dve", "pool"]
        for a in ce:
            ea = self.E[a]
            for b in ce:
                if a == b:
                    continue
                eb = self.E[b]
                if eb.cnt > 0 and ea.seen.get(id(eb.sem), 0) < eb.cnt:
                    ea.obj.wait_ge(eb.sem, eb.cnt)
                    ea.seen[id(eb.sem)] = eb.cnt
            for R in regions:
                evs = list(R.r.values()) + ([R.w] if R.w else [])
                for sem, val in evs:
                    if sem is ea.sem or ea.seen.get(id(sem), 0) >= val:
                        continue
                    ea.obj.wait_ge(sem, val)
                    ea.seen[id(sem)] = val

    def finish(self, regions):
        e = self.E["sp"]
        for R in regions:
            if R.w is not None:
                sem, val = R.w
                if e.seen.get(id(sem), 0) < val:
                    e.obj.wait_ge(sem, val)
                    e.seen[id(sem)] = val


class _Stop(Exception):
    pass


def build(T, dbg=None):
    NTL = T // NT
    NB = T // 128
    nc = bass.Bass("TRN2", target_bir_lowering=False)

    def din(name, shape, dt=F32):
        return nc.dram_tensor(name, list(shape), dt, kind="ExternalInput").ap()

    xT_d = din("xT", [D, T])
    cT_d = din("cT", [128, 8])
    pos_d = din("pos", [1, T], I32)
    wada_d = din("wada", [L, D, 6 * D])
    win_d = din("win", [L, D, 2400])
    wout_d = din("wout", [L, D, D])
    ff1_d = din("ff1", [L, D, 4 * D])
    ff2_d = din("ff2", [L, 4 * D, D])
    wl_d = din("wl", [L, 128, 1152])
    wuq_d = din("wuq", [L, 256, 576])
    wk_d = din("wk", [L, 256, 384])
    wv_d = din("wv", [L, 256, 384])
    cols_d = din("cols", [L, 128, NCOL])
    const_d = din("consts", [128, NCONST])
    oT_d = nc.dram_tensor("oT", [D, T], F32, kind="ExternalOutput").ap()
    kc_d = nc.dram_tensor("kcache", [L, 6, 96, T], BF16, kind="Internal").ap()
    vc_d = nc.dram_tensor("vcache", [L, 6, 128, NB, 64], BF16, kind="Internal").ap()
    dbg_d = None
    if dbg is not None:
        dbg_d = nc.dram_tensor("dbg", [128, 16 * 512], F32, kind="ExternalOutput").ap()
    dbg_slots = {}
    R_dbg = Region("dbg")

    es = ExitStack()
    with es:
        k = KB(nc, es)

        def sb(name, shape, dt=F32):
            return es.enter_context(nc.sbuf_tensor(name, list(shape), dt))

        def ps(name):
            return es.enter_context(nc.psum_tensor(name, [128, 512], F32))

        xT = sb("xT_s", [128, 8, NT]); R_x = [Region("x%d" % i) for i in range(8)]
        hT = sb("hT_s", [128, 8, NT], BF16); R_h = Region("hT")
        mixT = sb("mixT_s", [128, 8, NT], BF16); R_mix = [Region("mix%d" % i) for i in range(8)]
        big = sb("big_s", [128, 8192]); R_big = Region("big")
        pT = big[:, 0:5120].rearrange("p (c t) -> p c t", t=NT)
        hid = big[:].bitcast(BF16).rearrange("p (c t) -> p c t", t=NT)
        wbuf = [sb("wbuf%d" % i, [128, 8192], BF16) for i in range(2)]
        R_wb = [Region("wb%d" % i) for i in range(2)]
        wl = sb("wl_s", [128, L, 1152], BF16); wuq = sb("wuq_s", [128, L, 2, 576], BF16)
        wk = sb("wk_s", [128, L, 2, 384], BF16); wv = sb("wv_s", [128, L, 2, 384], BF16)
        R_sw = Region("smallw")
        cols = sb("cols_s", [128, L, NCOL]); R_cols = Region("cols")
        dcol = sb("dcol_s", [128, L, NDC]); R_dcol = Region("dcol")
        cst = sb("cst_s", [128, NCONST]); R_cst = Region("cst")
        cstb = sb("cstb_s", [128, 256], BF16)
        cs_c = sb("csc_s", [128, 8]); cs_b = sb("csb_s", [128, 8], BF16); R_cs = Region("cs")
        hbuf = [sb("hbuf%d" % l, [128, 2, 30 + NT], BF16) for l in range(L)]
        R_hb = [Region("hb%d" % l) for l in range(L)]
        plast = sb("plast_s", [128, L, 10]); R_pl = [Region("pl%d" % l) for l in range(L)]
        Sbd = [[sb("S%d_%d" % (l, p), [128, 128]) for p in range(3)] for l in range(L)]
        R_S = [[Region("S%d_%d" % (l, p)) for p in range(3)] for l in range(L)]
        kint = sb("kint_s", [128, NT], I32)
        cosT = sb("cos_s", [128, NT]); sinT = sb("sin_s", [128, NT]); R_cos = Region("cossin")
        arena = sb("arena_s", [128, 15104]); R_ar = Region("arena")
        expd = {nm: (sb("exp_" + nm, [128, 512]), Region("exp_" + nm)) for nm in ("Abd", "Bbd", "Kbd", "Vbd")}
        banks = [ps("bank%d" % i) for i in range(8)]
        R_bk = [Region("bank%d" % i) for i in range(8)]
        R_kc = [[Region("kc%d_%d" % (l, h)) for h in range(6)] for l in range(L)]
        R_vc = [[Region("vc%d_%d" % (l, h)) for h in range(6)] for l in range(L)]
        R_out = Region("out")

        ident = cst[:, K_ID:K_ID + 128]
        ones_m = cst[:, K_ONES:K_ONES + 128]
        blk2 = cst[:, K_BLK:K_BLK + 128]
        su2 = cst[:, K_SU:K_SU + 128]
        sl2 = cst[:, K_SL:K_SL + 128]
        uis = cst[:, K_UIS:K_UIS + 64]
        rmask = cst[:, K_RM:K_RM + 512]
        swapm = cst[:, K_SWAP:K_SWAP + 128]
        rotm = cst[:, K_ROT:K_ROT + 128]
        ifr = cst[:, K_IF:K_IF + 1]

        _bk = [0]

        def bank(pool=(2, 3, 4, 5, 6)):
            _bk[0] += 1
            i = pool[_bk[0] % len(pool)]
            return banks[i], R_bk[i]

        def mm(out, lhsT, rhs, start, stop, R, W, inc=True):
            k.op("pe", lambda e: e.matmul(out, lhsT, rhs, start=start, stop=stop), R, W, inc=inc)

        def act(out, in_, func, R, W, bias=0.0, scale=1.0):
            k.op("act", lambda e: e.activation(out=out, in_=in_, func=func, bias=bias, scale=scale), R, W)

        def tt(en, out, in0, in1, op, R, W):
            k.op(en, lambda e: e.tensor_tensor(out=out, in0=in0, in1=in1, op=op), R, W)

        def ts(en, out, in0, s1, s2, op0, op1, R, W):
            if op1 is None:
                k.op(en, lambda e: e.tensor_scalar(out=out, in0=in0, scalar1=s1, scalar2=None, op0=op0), R, W)
            else:
                k.op(en, lambda e: e.tensor_scalar(out=out, in0=in0, scalar1=s1, scalar2=s2, op0=op0, op1=op1), R, W)

        def stt(en, out, in0, scalar, in1, op0, op1, R, W):
            k.op(en, lambda e: e.scalar_tensor_tensor(out=out, in0=in0, scalar=scalar, in1=in1, op0=op0, op1=op1), R, W)

        def cp(en, out, in_, R, W):
            if en == "act":
                k.op("act", lambda e: e.copy(out=out, in_=in_), R, W)
            else:
                k.op(en, lambda e: e.tensor_copy(out=out, in_=in_), R, W)

        def recip(out, in_, R, W):
            k.op("dve", lambda e: e.reciprocal(out=out, in_=in_), R, W)

        def memset(en, ap, val, W):
            k.op(en, lambda e: e.memset(ap, val), (), W)

        def chk(name, ap, R):
            if dbg is None:
                return
            names, stop = dbg
            if name in names and name not in dbg_slots:
                i = len(dbg_slots)
                dbg_slots[name] = i
                n = ap.shape[-1]
                k.dma("pool", dbg_d[0:ap.shape[0], i * 512:i * 512 + n], ap, R, [R_dbg], accumulate=True)
            if name == stop:
                raise _Stop()

        k.dma("sp", cst[:], const_d, (), [R_cst])
        k.dma("sp", cols[:], cols_d.rearrange("l p n -> p l n"), (), [R_cols])
        k.dma("sp", cs_c[:], cT_d, (), [R_cs])
        R_sw2 = [Region("sw%d" % i) for i in range(4)]
        k.dma("pool", wl[:], wl_d.rearrange("l p n -> p l n"), (), [R_sw2[0]])
        k.dma("pool", wuq[:], wuq_d.rearrange("l (c p) n -> p l c n", p=128), (), [R_sw2[1]])
        k.dma("pool", wk[:], wk_d.rearrange("l (c p) n -> p l c n", p=128), (), [R_sw2[2]])
        k.dma("pool", wv[:], wv_d.rearrange("l (c p) n -> p l c n", p=128), (), [R_sw2[3]])
        R_SW = R_sw2
        cp("dve", cstb[:, 0:128], ident, [R_cst], [R_cst])
        cp("dve", cstb[:, 128:256], cst[:, K_CM:K_CM + 128], [R_cst], [R_cst])
        identb = cstb[:, 0:128]
        cmb = cstb[:, 128:256]
        for l in range(L):
            for p in range(3):
                memset("dve", Sbd[l][p][:], 0.0, [R_S[l][p]])
            memset("dve", hbuf[l][:], 0.0, [R_hb[l]])
            memset("dve", plast[:, l, :], 0.0, [R_pl[l]])
        for nm in expd:
            memset("dve", expd[nm][0][:], 0.0, [expd[nm][1]])
        act(cs_c[:], cs_c[:], AF.Silu, [R_cs], [R_cs])
        cp("dve", cs_b[:], cs_c[:], [R_cs], [R_cs])

        _wb = [0]

        def wload(src_ap, shape_view):
            i = _wb[0] % 2
            _wb[0] += 1
            n = 1
            for s in shape_view[1:]:
                n *= s
            flat = wbuf[i][:, 0:n]
            if len(shape_view) == 3:
                view = flat.rearrange("p (a b) -> p a b", b=shape_view[2])
            else:
                view = flat
            k.dma("pool", view, src_ap, (), [R_wb[i]])
            return view, R_wb[i]

        for l in range(L):
            for g in range(8):
                wv_, Rw = wload(wada_d[l, :, g * 768:(g + 1) * 768].rearrange("(c p) n -> p c n", p=128), [128, 8, 768])
                bk, Rb = bank()
                for j in range(6):
                    for kc in range(8):
                        mm(bk[:, j:j + 1], wv_[:, kc, j * 128:(j + 1) * 128], cs_b[:, kc:kc + 1], kc == 0, kc == 7,
                           [Rw, R_cs], [Rb])
                tt("dve", dcol[:, l, MOD + g * 6:MOD + g * 6 + 6], bk[:, 0:6], cols[:, l, BADA + g * 6:BADA + g * 6 + 6],
                   ALU.add, [Rb, R_cols], [R_dcol])
            m = lambda i: dcol[:, l, MOD + i * 8:MOD + i * 8 + 8]
            stt("dve", dcol[:, l, GM1:GM1 + 8], m(1), 1.0, cols[:, l, GPM:GPM + 8], ALU.add, ALU.mult, [R_dcol, R_cols], [R_dcol])
            tt("dve", dcol[:, l, GG1:GG1 + 8], m(2), cols[:, l, GQM:GQM + 8], ALU.mult, [R_dcol, R_cols], [R_dcol])
            stt("dve", dcol[:, l, GM2:GM2 + 8], m(4), 1.0, cols[:, l, GPF:GPF + 8], ALU.add, ALU.mult, [R_dcol, R_cols], [R_dcol])
            tt("dve", dcol[:, l, GG2:GG2 + 8], m(5), cols[:, l, GQF:GQF + 8], ALU.mult, [R_dcol, R_cols], [R_dcol])
            ts("dve", dcol[:, l, OMU:OMU + 10], cols[:, l, MU:MU + 10], -1.0, 1.0, ALU.mult, ALU.add, [R_cols], [R_dcol])
            ts("dve", dcol[:, l, OMKA:OMKA + 3], cols[:, l, KA:KA + 3], -1.0, 1.0, ALU.mult, ALU.add, [R_cols], [R_dcol])

        def col(l, base, i=0):
            return cols[:, l, base + i:base + i + 1]

        def dc(l, base, i=0):
            return dcol[:, l, base + i:base + i + 1]

        def rms_rstd(out, srcs, Rsrc, nfeat, eps, sq_tmp, R_tmp, lhs=None):
            bk, Rb = bank((0, 1))
            n = len(srcs)
            for i, s in enumerate(srcs):
                st = sq_tmp[i % len(sq_tmp)]
                act(st, s, AF.Square, Rsrc, [R_tmp[i % len(sq_tmp)]])
                mm(bk[:], lhs if lhs is not None else ones_m, st, i == 0, i == n - 1, [R_tmp[i % len(sq_tmp)], R_cst], [Rb])
            act(out[0], bk[:], AF.Ln, [Rb], [out[1]], bias=eps, scale=1.0 / nfeat)
            act(out[0], out[0], AF.Exp, [out[1]], [out[1]], scale=-0.5)

        def carve(specs):
            off = 0
            d = {}
            for name, n in specs:
                d[name] = (arena[:, off:off + n], Region("ar_" + name))
                off += n
            assert off <= 15104, off
            return d

        try:
            for t in range(NTL):
                tok = slice(t * NT, (t + 1) * NT)
                for c in range(8):
                    k.dma("sp", xT[:, c, :], xT_d[c * 128:(c + 1) * 128, tok], (), [R_x[c]])
                for l in range(L):
                    k.barrier([R_ar])
                    A = carve([("t%d" % i, 512) for i in range(19)] +
                              [("BT", 512), ("KT", 512), ("VT", 512),
                               ("Am", 512), ("ATm", 512), ("Tm", 512), ("Mka", 512), ("Mbr", 256), ("Mkr", 256),
                               ("preU", 256), ("U", 256), ("la", 256), ("rstd", 512)])
                    tmp = [A["t%d" % i] for i in range(19)]
                    rstd = A["rstd"]
                    rms_rstd(rstd, [xT[:, c, :] for c in range(8)], R_x, D, 1e-6, [tmp[0][0], tmp[1][0]], [tmp[0][1], tmp[1][1]])
                    for c in range(8):
                        stt("dve", tmp[c % 2][0], xT[:, c, :], dc(l, GM1, c), rstd[0], ALU.mult, ALU.mult,
                            [R_x[c], rstd[1], R_dcol], [tmp[c % 2][1]])
                        act(hT[:, c, :], tmp[c % 2][0], AF.Identity, [tmp[c % 2][1], R_dcol], [R_h], bias=dc(l, MOD, c))
                    if t == 0 and l == 0:
                        chk('mod', dcol[:, 0, :], [R_dcol])
                        chk('h', hT[:, 0, :], [R_h])
                    for pc in range(2):
                        wv_, Rw = wload(win_d[l, :, pc * 640:(pc + 1) * 640].rearrange("(c p) n -> p c n", p=128), [128, 8, 640])
                        for j in range(5):
                            oc = pc * 5 + j
                            bk, Rb = bank((0, 1))
                            for kc in range(8):
                                mm(bk[:], wv_[:, kc, j * 128:(j + 1) * 128], hT[:, kc, :], kc == 0, kc == 7, [Rw, R_h], [Rb], inc=(kc == 7))
                            cp("act", pT[:, oc, :], bk[:], [Rb], [R_big])
                    if t == 0 and l == 0:
                        chk('p0', pT[:, 0, :], [R_big])
                    for c in range(10):
                        tm_, Rt = tmp[c % 2]
                        ts("dve", tm_[:, 0:NT - 1], pT[:, c, 0:NT - 1], col(l, MU, c), None, ALU.mult, None, [R_big, R_cols], [Rt])
                        ts("dve", tm_[:, NT - 1:NT], pT[:, c, NT - 1:NT], 1.0, None, ALU.mult, None, [R_big], [Rt])
                        act(pT[:, c, :], pT[:, c, :], AF.Identity, [R_big, R_dcol], [R_big], scale=dc(l, OMU, c))
                        tt("dve", pT[:, c, 1:NT], pT[:, c, 1:NT], tm_[:, 0:NT - 1], ALU.add, [R_big, Rt], [R_big])
                        stt("dve", pT[:, c, 0:1], plast[:, l, c:c + 1], col(l, MU, c), pT[:, c, 0:1], ALU.mult, ALU.add,
                            [R_big, R_pl[l], R_cols], [R_big])
                        cp("dve", plast[:, l, c:c + 1], tm_[:, NT - 1:NT], [Rt], [R_pl[l]])
                    zT = pT
                    if t == 0 and l == 0:
                        chk('z0', pT[:, 0, :], [R_big])
                        chk('z9', pT[:, 9, :], [R_big])
                    la = A["la"][0].bitcast(BF16)
                    R_la = A["la"][1]
                    act(la[0:32, :], zT[0:32, 9, :], AF.Tanh, [R_big], [R_la])
                    act(la[64:128, :], zT[64:128, 9, :], AF.Sigmoid, [R_big], [R_la])
                    cp("dve", la[32:64, :], zT[32:64, 9, :], [R_big], [R_la])
                    def prep_gen(pr):
                        sg, cum, Pinv, a_t, tx = tmp[0:5]
                        Pin, Pex, g_t, kk_, kp_, bon, Rt_ = tmp[5 + 7 * (pr % 2):12 + 7 * (pr % 2)]
                        rT, kT_, vT_ = zT[:, pr, :], zT[:, 3 + pr, :], zT[:, 6 + pr, :]
                        bk, Rb = bank()
                        mm(bk[:], wl[:, l, pr * 128:(pr + 1) * 128], la, True, True, [R_SW[0], R_la], [Rb])
                        act(sg[0], bk[:], AF.Sigmoid, [Rb, R_cols], [sg[1]], bias=col(l, W0, pr))
                        yield
                        k.op("dve", lambda e: e.tensor_tensor_scan(out=cum[0], data0=rmask, data1=sg[0], initial=0.0,
                                                                   op0=ALU.mult, op1=ALU.add), [sg[1], R_cst], [cum[1]])
                        yield
                        act(Pin[0], cum[0], AF.Exp, [cum[1]], [Pin[1]], scale=-C0)
                        yield
                        act(Pinv[0], cum[0], AF.Exp, [cum[1]], [Pinv[1]], scale=C0)
                        yield
                        tt("dve", sg[0], cum[0], sg[0], ALU.subtract, [cum[1], sg[1]], [sg[1]])
                        yield
                        act(Pex[0], sg[0], AF.Exp, [sg[1]], [Pex[1]], scale=-C0)
                        yield
                        bk, Rb = bank()
                        mm(bk[:], wl[:, l, 384 + pr * 128:384 + (pr + 1) * 128], la, True, True, [R_SW[0], R_la], [Rb])
                        act(a_t[0], bk[:], AF.Sigmoid, [Rb, R_cols], [a_t[1]], bias=col(l, A0, pr))
                        yield
                        bk, Rb = bank()
                        mm(bk[:], wl[:, l, 768 + pr * 128:768 + (pr + 1) * 128], la, True, True, [R_SW[0], R_la], [Rb])
                        cp("act", g_t[0], bk[:], [Rb], [g_t[1]])
                        yield
                        ts("dve", kk_[0], kT_, col(l, KK, pr), None, ALU.mult, None, [R_big, R_cols], [kk_[1]])
                        yield
                        act(tx[0], kk_[0], AF.Square, [kk_[1]], [tx[1]])
                        yield
                        bk, Rb = bank()
                        mm(bk[:], blk2, tx[0], True, True, [tx[1], R_cst], [Rb])
                        ts("dve", tx[0], bk[:], 1e-24, None, ALU.max, None, [Rb], [tx[1]])
                        yield
                        act(tx[0], tx[0], AF.Ln, [tx[1]], [tx[1]])
                        yield
                        act(tx[0], tx[0], AF.Exp, [tx[1]], [tx[1]], scale=-0.5)
                        yield
                        tt("dve", kk_[0], kk_[0], tx[0], ALU.mult, [kk_[1], tx[1]], [kk_[1]])
                        yield
                        ts("dve", kp_[0], a_t[0], col(l, KA, pr), dc(l, OMKA, pr), ALU.mult, ALU.add, [a_t[1], R_cols, R_dcol], [kp_[1]])
                        yield
                        tt("dve", kp_[0], kp_[0], kT_, ALU.mult, [kp_[1], R_big], [kp_[1]])
                        yield
                        stt("dve", tx[0], rT, col(l, RK, pr), kp_[0], ALU.mult, ALU.mult, [R_big, kp_[1], R_cols], [tx[1]])
                        yield
                        bk, Rb = bank()
                        mm(bk[:], blk2, tx[0], True, True, [tx[1], R_cst], [Rb])
                        tt("dve", bon[0], bk[:], vT_, ALU.mult, [Rb, R_big], [bon[1]])
                        yield
                        tt("dve", Rt_[0], rT, Pin[0], ALU.mult, [R_big, Pin[1]], [Rt_[1]])
                        yield
                        stt("dve", Pex[0], kk_[0], -1.0, Pex[0], ALU.mult, ALU.mult, [kk_[1], Pex[1]], [Pex[1]])
                        yield
                        tt("dve", kk_[0], kk_[0], a_t[0], ALU.mult, [kk_[1], a_t[1]], [kk_[1]])
                        yield
                        tt("dve", kk_[0], kk_[0], Pinv[0], ALU.mult, [kk_[1], Pinv[1]], [kk_[1]])
                        yield
                        tt("dve", kp_[0], kp_[0], Pinv[0], ALU.mult, [kp_[1], Pinv[1]], [kp_[1]])
                        yield
                        return

                    nxt_gen = [None]

                    def pump(n):
                        g = nxt_gen[0]
                        if g is None:
                            return
                        for _ in range(n):
                            try:
                                next(g)
                            except StopIteration:
                                nxt_gen[0] = None
                                return

                    for _ in prep_gen(0):
                        pass
                    for pr in range(3):
                        sg, cum, Pinv, a_t, tx = tmp[0:5]
                        Pin, Pex, g_t, kk_, kp_, bon, Rt_ = tmp[5 + 7 * (pr % 2):12 + 7 * (pr % 2)]
                        rT, kT_, vT_ = zT[:, pr, :], zT[:, 3 + pr, :], zT[:, 6 + pr, :]
                        nxt_gen[0] = prep_gen(pr + 1) if pr < 2 else None
                        ybk, Rybk = banks[7], R_bk[7]
                        for half in range(2):
                            hs = slice(half * 256, (half + 1) * 256)
                            exp_ = {}
                            for nm, src, Rs in (("Abd", Pex[0], Pex[1]), ("Bbd", kk_[0], kk_[1]), ("Kbd", kp_[0], kp_[1]), ("Vbd", vT_, R_big)):
                                dst, Rd = expd[nm]
                                d3 = dst[:].rearrange("p (c n) -> p c n", n=128)
                                s3 = src[:, hs].rearrange("p (c n) -> p c n", n=64)
                                cp("act", d3[0:64, :, 0:64], s3[0:64], [Rs], [Rd])
                                cp("dve", d3[64:128, :, 64:128], s3[64:128], [Rs], [Rd])
                                exp_[nm] = (d3, Rd)
                            for nm, src in (("BT", "Bbd"), ("KT", "Kbd"), ("VT", "Vbd")):
                                bk, Rb = bank()
                                for c in range(4):
                                    k.op("pe", lambda e, c=c, bk=bk, src=src: e.transpose(bk[:, c * 128:(c + 1) * 128], exp_[src][0][:, c, :], ident),
                                         [exp_[src][1], R_cst], [Rb])
                                cp("act", A[nm][0], bk[:], [Rb], [A[nm][1]])
                            BT3 = A["BT"][0].rearrange("p (c n) -> p c n", n=128)
                            KT3 = A["KT"][0].rearrange("p (c n) -> p c n", n=128)
                            VT3 = A["VT"][0].rearrange("p (c n) -> p c n", n=128)
                            Abd3, Bbd3, Kbd3 = exp_["Abd"][0], exp_["Bbd"][0], exp_["Kbd"][0]
                            R_Abd, R_Bbd, R_Kbd = exp_["Abd"][1], exp_["Bbd"][1], exp_["Kbd"][1]
                            su4 = su2.unsqueeze(1).to_broadcast([128, 4, 128])
                            sl4 = sl2.unsqueeze(1).to_broadcast([128, 4, 128])
                            ui4 = uis.unsqueeze(1).to_broadcast([128, 4, 64])

                            def gram(dst, lhs3, Rl, rhs3, Rr, mask4, n):
                                bk, Rb = bank()
                                for c in range(4):
                                    mm(bk[:, c * n:(c + 1) * n], lhs3[:, c, :], rhs3[:, c, :], True, True, [Rl, Rr], [Rb])
                                tt("dve", dst[0].rearrange("p (c n) -> p c n", n=n), bk[:, 0:4 * n].rearrange("p (c n) -> p c n", n=n),
                                   mask4, ALU.mult, [Rb, R_cst], [dst[1]])

                            gram(A["Am"], Bbd3, R_Bbd, Abd3, R_Abd, su4, 128)
                            pump(2)
                            gram(A["ATm"], Abd3, R_Abd, Bbd3, R_Bbd, sl4, 128)
                            pump(2)
                            gram(A["Mka"], Kbd3, R_Kbd, Abd3, R_Abd, su4, 128)
                            pump(2)
                            Rt3 = Rt_[0][:, hs].rearrange("p (c n) -> p c n", n=64)
                            gram(A["Mbr"], Bbd3, R_Bbd, Rt3, Rt_[1], ui4, 64)
                            pump(2)
                            gram(A["Mkr"], Kbd3, R_Kbd, Rt3, Rt_[1], ui4, 64)
                            pump(2)
                            Am3 = A["Am"][0].rearrange("p (c n) -> p c n", n=128)
                            ATm3 = A["ATm"][0].rearrange("p (c n) -> p c n", n=128)
                            Tm3 = A["Tm"][0].rearrange("p (c n) -> p c n", n=128)
                            Mka3 = A["Mka"][0].rearrange("p (c n) -> p c n", n=128)
                            Mbr3 = A["Mbr"][0].rearrange("p (c n) -> p c n", n=64)
                            Mkr3 = A["Mkr"][0].rearrange("p (c n) -> p c n", n=64)
                            R_Am, R_ATm, R_Tm = A["Am"][1], A["ATm"][1], A["Tm"][1]
                            tt("dve", Tm3, Am3, ident.unsqueeze(1).to_broadcast([128, 4, 128]), ALU.add, [R_Am, R_cst], [R_Tm])
                            for j in range(1, 6):
                                bkA, RbA = bank()
                                bkT, RbT = bank()
                                if j < 5:
                                    for c in range(4):
                                        mm(bkA[:, c * 128:(c + 1) * 128], ATm3[:, c, :], Am3[:, c, :], True, True, [R_Am, R_ATm], [RbA])
                                for c in range(4):
                                    mm(bkT[:, c * 128:(c + 1) * 128], Am3[:, c, :], ATm3[:, c, :], True, True, [R_Am, R_ATm], [RbT])
                                if j < 5:
                                    cp("act", A["Am"][0], bkA[:], [RbA], [R_Am])
                                cp("dve", A["ATm"][0], bkT[:], [RbT], [R_ATm])
                                bkU, RbU = bank()
                                for c in range(4):
                                    mm(bkU[:, c * 128:(c + 1) * 128], ATm3[:, c, :], Tm3[:, c, :], True, True, [R_ATm, R_Tm], [RbU])
                                tt("dve", A["Tm"][0], A["Tm"][0], bkU[:], ALU.add, [R_Tm, RbU], [R_Tm])
                                pump(3)
                            S_, RS_ = Sbd[l][pr], R_S[l][pr]
                            preU, U_ = A["preU"], A["U"]
                            for c in range(4):
                                cg = half * 4 + c
                                bk, Rb = bank()
                                mm(bk[:, 0:128], Abd3[:, c, :], S_[:], True, False, [R_Abd, RS_], [Rb])
                                mm(bk[:, 0:128], Mka3[:, c, :], VT3[:, c, :], False, True, [A["Mka"][1], A["VT"][1]], [Rb])
                                cp("act", preU[0][:, 0:128], bk[:, 0:128], [Rb], [preU[1]])
                                bk2, Rb2 = bank()
                                mm(bk2[:, 0:128], Tm3[:, c, :], preU[0][:, 0:128], True, True, [R_Tm, preU[1]], [Rb2])
                                cp("dve", U_[0][:, 0:128], bk2[:, 0:128], [Rb2], [U_[1]])
                                yc = ybk[:, cg * 64:(cg + 1) * 64]
                                mm(yc, S_[:], Rt_[0][:, cg * 64:(cg + 1) * 64], True, False, [RS_, Rt_[1]], [Rybk])
                                mm(yc, U_[0][:, 0:128], Mbr3[:, c, :], False, False, [U_[1], A["Mbr"][1]], [Rybk])
                                mm(yc, VT3[:, c, :], Mkr3[:, c, :], False, True, [A["VT"][1], A["Mkr"][1]], [Rybk])
                                bk3, Rb3 = bank()
                                mm(bk3[:, 0:128], ident, S_[:], True, False, [R_cst, RS_], [Rb3])
                                mm(bk3[:, 0:128], BT3[:, c, :], U_[0][:, 0:128], False, False, [A["BT"][1], U_[1]], [Rb3])
                                mm(bk3[:, 0:128], KT3[:, c, :], VT3[:, c, :], False, True, [A["KT"][1], A["VT"][1]], [Rb3])
                                ts("dve", S_[:], bk3[:, 0:128], Pin[0][:, cg * 64 + 63:cg * 64 + 64], None, ALU.mult, None,
                                   [Rb3, Pin[1]], [RS_])
                                pump(3)
                        pump(10 ** 9)
                        ysb = sg
                        cp("act", ysb[0], ybk[:], [Rybk], [ysb[1]])
                        if t == 0 and l == 0 and pr == 0:
                            chk('y', ysb[0], [ysb[1]])
                        bk, Rb = bank()
                        mm(bk[:], blk2, ysb[0], True, True, [ysb[1], R_cst], [Rb])
                        stt("dve", ysb[0], bk[:], -1.0 / 64, ysb[0], ALU.mult, ALU.add, [Rb, ysb[1]], [ysb[1]])
                        act(tx[0], ysb[0], AF.Square, [ysb[1]], [tx[1]])
                        bk, Rb = bank()
                        mm(bk[:], blk2, tx[0], True, True, [tx[1], R_cst], [Rb])
                        act(tx[0], bk[:], AF.Ln, [Rb], [tx[1]], bias=64e-5, scale=1.0 / 64)
                        act(tx[0], tx[0], AF.Exp, [tx[1]], [tx[1]], scale=-0.5)
                        tt("dve", ysb[0], ysb[0], tx[0], ALU.mult, [ysb[1], tx[1]], [ysb[1]])
                        ts("dve", ysb[0], ysb[0], col(l, LNW, pr), col(l, LNB, pr), ALU.mult, ALU.add, [ysb[1], R_cols], [ysb[1]])
                        tt("dve", ysb[0], ysb[0], bon[0], ALU.add, [ysb[1], bon[1]], [ysb[1]])
                        tt("dve", mixT[:, pr, :], ysb[0], g_t[0], ALU.mult, [ysb[1], g_t[1]], [R_mix[pr]])
                        if t == 0 and l == 0 and pr == 0:
                            chk('ya', mixT[:, 0, :], [R_mix[0]])

                    k.barrier([R_ar])
                    Bm = carve([("dg", 512)] + [("f%d" % i, 512) for i in range(5)] + [("acc0", 512), ("acc1", 512), ("rq", 512), ("rkv", 512),
                               ("qn", 512), ("kvn", 512), ("qT", 1536), ("kTn", 1536), ("vnew", 768),
                               ("kb0", 512), ("kb1", 512), ("vb0", 512), ("vb1", 512), ("pt0", 256), ("pt1", 256), ("pt2", 256),
                               ("osb", 512), ("rde", 512), ("rdo", 512)])
                    ft = [Bm["f%d" % i] for i in range(5)]
                    for pc, (c0_, ncol, nch) in enumerate(((1280, 640, 5), (1920, 480, 4))):
                        wv_, Rw = wload(win_d[l, :, c0_:c0_ + ncol].rearrange("(c p) n -> p c n", p=128), [128, 8, ncol])
                        for j in range(nch):
                            oc = pc * 5 + j
                            M = 96 if oc == 8 else 128
                            bk, Rb = bank((0, 1))
                            for kc in range(8):
                                mm(bk[0:M, :], wv_[:, kc, j * 128:j * 128 + M], hT[:, kc, :], kc == 0, kc == 7, [Rw, R_h], [Rb], inc=(kc == 7))
                            cp("act", pT[0:M, oc, :], bk[0:M, :], [Rb], [R_big])
                    hb, Rhb = hbuf[l], R_hb[l]
                    for ch in range(2):
                        act(ft[0][0], pT[:, 2 + ch, :], AF.Sigmoid, [R_big], [ft[0][1]])
                        tt("dve", hb[:, ch, 30:30 + NT], pT[:, ch, :], ft[0][0], ALU.mult, [R_big, ft[0][1]], [Rhb])
                    accs = [Bm["acc0"], Bm["acc1"]]
                    dgv = Bm["dg"][0].bitcast(BF16).rearrange("p (i n) -> p i n", n=128)
                    R_dg = [Region("dg%d" % i) for i in range(8)]
                    di = 0
                    for ch in range(2):
                        acc, Racc = accs[ch]
                        bk, Rb = bank((0, 1))
                        for j in range(31):
                            sl = di % 8
                            di += 1
                            if sl % 2 == 0:
                                ts("dve", dgv[:, sl, :], identb, col(l, CW, j * 2 + ch), None, ALU.mult, None, [R_cst, R_cols], [R_dg[sl]])
                            else:
                                act(dgv[:, sl, :], identb, AF.Identity, [R_cst, R_cols], [R_dg[sl]], scale=col(l, CW, j * 2 + ch))
                            mm(bk[:], dgv[:, sl, :], hb[:, ch, j:j + NT], j == 0, j == 30, [R_dg[sl], Rhb], [Rb])
                        act(acc, bk[:], AF.Identity, [Rb, R_cols], [Racc], bias=col(l, CB, ch))
                    for ch in range(2):
                        cp("act", hb[:, ch, 0:30], hb[:, ch, NT:NT + 30], [Rhb] + [a[1] for a in accs], [Rhb])
                    bk, Rb = bank()
                    for ch in range(2):
                        mm(bk[:], ones_m, accs[ch][0], ch == 0, ch == 1, [accs[ch][1], R_cst], [Rb])
                    for ch in range(2):
                        stt("dve", accs[ch][0], bk[:], -1.0 / 256, accs[ch][0], ALU.mult, ALU.add, [Rb, accs[ch][1]], [accs[ch][1]])
                    rms_rstd(ft[2], [accs[0][0], accs[1][0]], [accs[0][1], accs[1][1]], 256, 1e-5, [ft[0][0], ft[1][0]], [ft[0][1], ft[1][1]])
                    for ch in range(2):
                        tt("dve", accs[ch][0], accs[ch][0], ft[2][0], ALU.mult, [accs[ch][1], ft[2][1]], [accs[ch][1]])
                        ts("dve", accs[ch][0], accs[ch][0], col(l, CLW, ch), col(l, CLB, ch), ALU.mult, ALU.add, [accs[ch][1], R_cols], [accs[ch][1]])
                        act(mixT[:, 3 + ch, :], accs[ch][0], AF.Silu, [accs[ch][1]], [R_mix[3 + ch]])
                    if t == 0 and l == 0:
                        chk('yb', mixT[:, 3, :], [R_mix[3]])
                    if l == 0:
                        k.dma("pool", ft[4][0][64:96, :], pos_d[:, tok].partition_broadcast(32).rearrange("p o t -> p (o t)"), (), [ft[4][1]])
                        ang = ft[4][0][64:96, :]
                        ts("dve", ang, ang, ifr[64:96, :], None, ALU.mult, None, [ft[4][1], R_cst], [ft[4][1]])
                        chk('ang0', ang, [ft[4][1]])
                        kf = ft[3][0][64:96, :]
                        ki = kint[64:96, :]
                        ts("dve", kf, ang, float(1.0 / (2 * np.pi)), None, ALU.mult, None, [ft[4][1]], [ft[3][1]])
                        cp("dve", ki, kf, [ft[3][1]], [ft[2][1]])
                        cp("dve", kf, ki, [ft[2][1]], [ft[3][1]])
                        chk('kf', kf, [ft[3][1]])
                        for cc in (6.28125, 1.9350051879882812e-03, 3.0199159819567e-07):
                            stt("dve", ang, kf, -cc, ang, ALU.mult, ALU.add, [ft[3][1], ft[4][1]], [ft[4][1]])
                        ts("dve", kf, ang, float(np.pi), float(-2 * np.pi), ALU.is_gt, ALU.mult, [ft[4][1]], [ft[3][1]])
                        tt("dve", ang, ang, kf, ALU.add, [ft[4][1], ft[3][1]], [ft[4][1]])
                        ts("dve", kf, ang, float(-np.pi), float(2 * np.pi), ALU.is_lt, ALU.mult, [ft[4][1]], [ft[3][1]])
                        tt("dve", ang, ang, kf, ALU.add, [ft[4][1], ft[3][1]], [ft[4][1]])
                        chk('red', ang, [ft[4][1]])
                        act(sinT[64:96, :], ang, AF.Sin, [ft[4][1]], [R_cos])
                        act(kf, ang, AF.Sin, [ft[4][1]], [ft[3][1]], scale=0.5)
                        tt("dve", kf, kf, kf, ALU.mult, [ft[3][1]], [ft[3][1]])
                        ts("dve", cosT[64:96, :], kf, -2.0, 1.0, ALU.mult, ALU.add, [ft[3][1]], [R_cos])
                        if t == 0:
                            chk('cos', cosT[64:96, :], [R_cos])
                            chk('sin', sinT[64:96, :], [R_cos])
                    rq, rkv = Bm["rq"], Bm["rkv"]
                    rms_rstd(rq, [pT[:, 4, :], pT[:, 5, :]], [R_big], 256, 1e-6, [ft[0][0], ft[1][0]], [ft[0][1], ft[1][1]])
                    rms_rstd(rkv, [pT[:, 6, :], pT[:, 7, :]], [R_big], 256, 1e-6, [ft[0][0], ft[1][0]], [ft[0][1], ft[1][1]])
                    qn = Bm["qn"][0].bitcast(BF16).rearrange("p (c t) -> p c t", t=NT)
                    kvn = Bm["kvn"][0].bitcast(BF16).rearrange("p (c t) -> p c t", t=NT)
                    for c in range(2):
                        stt("dve", qn[:, c, :], pT[:, 4 + c, :], col(l, QN, c), rq[0], ALU.mult, ALU.mult, [R_big, rq[1], R_cols], [Bm["qn"][1]])
                        stt("dve", kvn[:, c, :], pT[:, 6 + c, :], col(l, KVN, c), rkv[0], ALU.mult, ALU.mult, [R_big, rkv[1], R_cols], [Bm["kvn"][1]])
                    qT = Bm["qT"][0].bitcast(BF16).rearrange("p (h t) -> p h t", t=NT)
                    kTn = Bm["kTn"][0].bitcast(BF16).rearrange("p (h t) -> p h t", t=NT)
                    vnew = Bm["vnew"][0].bitcast(BF16).rearrange("p (b n) -> p b n", n=384)
                    R_qT, R_kTn, R_vnew = Bm["qT"][1], Bm["kTn"][1], Bm["vnew"][1]

                    def rope(dst, raw, Rraw, Rdst):
                        bk, Rb = bank()
                        mm(bk[0:96, :], rotm[0:96, 0:96], raw[0:96, :], True, True, [Rraw, R_cst], [Rb])
                        tt("dve", ft[1][0][64:96, :], bk[64:96, :], sinT[64:96, :], ALU.mult, [Rb, R_cos], [ft[1][1]])
                        tt("dve", ft[2][0][64:96, :], raw[64:96, :], cosT[64:96, :], ALU.mult, [Rraw, R_cos], [ft[2][1]])
                        tt("dve", dst, ft[1][0][64:96, :], ft[2][0][64:96, :], ALU.add, [ft[1][1], ft[2][1]], [Rdst])

                    kpe, Rkpe = ft[3]
                    rope(kpe.bitcast(BF16)[64:96, 0:NT], pT[:, 8, :], R_big, Rkpe)
                    for h in range(6):
                        bk, Rb = bank()
                        for c in range(2):
                            mm(bk[0:96, :], wuq[:, l, c, h * 96:(h + 1) * 96], qn[:, c, :], c == 0, c == 1, [R_SW[1], Bm["qn"][1]], [Rb], inc=(c == 1))
                        cp("act", ft[0][0][0:96, :], bk[0:96, :], [Rb], [ft[0][1]])
                        cp("act", qT[0:64, h, :], bk[0:64, :], [Rb], [R_qT])
                        rope(qT[64:96, h, :], ft[0][0], ft[0][1], R_qT)
                        bk, Rb = bank()
                        for c in range(2):
                            mm(bk[0:64, :], wk[:, l, c, h * 64:(h + 1) * 64], kvn[:, c, :], c == 0, c == 1, [R_SW[2], Bm["kvn"][1]], [Rb], inc=(c == 1))
                        cp("act", kTn[0:64, h, :], bk[0:64, :], [Rb], [R_kTn])
                        cp("dve", kTn[64:96, h, :], kpe.bitcast(BF16)[64:96, 0:NT], [Rkpe], [R_kTn])
                    for b4 in range(4):
                        bk, Rb = bank()
                        for c in range(2):
                            mm(bk[:, 0:384], kvn[:, c, b4 * 128:(b4 + 1) * 128], wv[:, l, c, :], c == 0, c == 1, [R_SW[3], Bm["kvn"][1]], [Rb], inc=(c == 1))
                        cp("act", vnew[:, b4, :], bk[:, 0:384], [Rb], [R_vnew])
                    if t == 0 and l == 0:
                        chk('q0', qT[0:96, 0, :], [R_qT])
                        chk('k0', kTn[0:96, 0, :], [R_kTn])
                        chk('v0', vnew[:, 0, :], [R_vnew])
                    for h in range(6):
                        k.dma("sp", kc_d[l, h, :, tok], kTn[0:96, h, :], [R_kTn], [R_kc[l][h]])
                        k.dma("sp", vc_d[l, h, :, t * 4:(t + 1) * 4, :], vnew[:, :, h * 64:(h + 1) * 64], [R_vnew], [R_vc[l][h]])
                    kbs = [Bm["kb0"], Bm["kb1"]]
                    vbs = [Bm["vb0"], Bm["vb1"]]
                    pts = [Bm["pt0"], Bm["pt1"], Bm["pt2"]]
                    osb = Bm["osb"]
                    rds = [Bm["rde"], Bm["rdo"]]
                    for i in range(2):
                        memset("dve", vbs[i][0].bitcast(BF16), 1.0, [vbs[i][1]])
                        memset("dve", rds[i][0], 0.0, [rds[i][1]])
                    nseg = t + 1
                    scale = float(96 ** -0.5)
                    blocks = [(h, s_, jb) for h in range(6) for s_ in range(nseg) for jb in range(4)]
                    segs = {}
                    qk = {}
                    li = [0]

                    def emit_qk(i):
                        h, s_, jb = blocks[i]
                        par = h % 2
                        if (h, s_) not in segs:
                            kb, Rkb = kbs[li[0] % 2]
                            vb, Rvb = vbs[li[0] % 2]
                            li[0] += 1
                            kbv = kb.bitcast(BF16)[0:96, 0:NT]
                            vbv = vb.bitcast(BF16).rearrange("p (b n) -> p b n", n=128)
                            k.dma("sp", kbv, kc_d[l, h, :, s_ * NT:(s_ + 1) * NT], [R_kc[l][h]], [Rkb])
                            if par == 0:
                                k.dma("sp", vbv[:, 0:4, 0:64], vc_d[l, h, :, s_ * 4:(s_ + 1) * 4, :], [R_vc[l][h]], [Rvb])
                            else:
                                k.dma("sp", vbv[:, 4:8, 64:128], vc_d[l, h, :, s_ * 4:(s_ + 1) * 4, :], [R_vc[l][h]], [Rvb])
                            segs[(h, s_)] = (kbv, Rkb, vbv, Rvb)
                        kbv, Rkb, vbv, Rvb = segs[(h, s_)]
                        diag = (s_ == t)
                        q0 = jb * 128 if diag else 0
                        nq = NT - q0
                        sbk, Rsb = bank()
                        mm(sbk[:, 0:nq], kbv[:, jb * 128:(jb + 1) * 128], qT[0:96, h, q0:NT], True, True, [Rkb, R_qT], [Rsb])
                        qk[i] = (sbk, Rsb, q0, nq, diag)

                    LA = 2
                    nxt = 0
                    for i in range(len(blocks)):
                        while nxt <= min(i + LA, len(blocks) - 1):
                            emit_qk(nxt)
                            nxt += 1
                        h, s_, jb = blocks[i]
                        par = h % 2
                        sbk, Rsb, q0, nq, diag = qk.pop(i)
                        kbv, Rkb, vbv, Rvb = segs[(h, s_)]
                        obk, Robk = banks[h % 2], R_bk[h % 2]
                        pt, Rpt = pts[i % 3]
                        ptv = pt.bitcast(BF16)[:, 0:NT]
                        act(ptv[:, 0:nq], sbk[:, 0:nq], AF.Exp, [Rsb], [Rpt], scale=scale)
                        if diag:
                            tt("dve", ptv[:, 0:128], ptv[:, 0:128], cmb, ALU.mult, [Rpt, R_cst], [Rpt])
                        first = (s_ == 0 and jb == 0)
                        last = (s_ == nseg - 1) and (jb == 3)
                        vsl = vbv[:, jb, :] if par == 0 else vbv[:, 4 + jb, :]
                        mm(obk[:, q0:NT], vsl, ptv[:, 0:nq], first, last, [Rvb, Rpt], [Robk], inc=True)
                        if last:
                            olo, dlo = (0, 64) if par == 0 else (64, 0)
                            cp("act", osb[0][olo:olo + 64, :], obk[olo:olo + 64, :], [Robk], [osb[1]])
                            rd, Rrd = rds[par]
                            recip(rd[dlo:dlo + 64, :], obk[dlo:dlo + 64, :], [Robk], [Rrd])
                            bk, Rb = bank()
                            mm(bk[:], swapm, rd, True, True, [Rrd, R_cst], [Rb])
                            tt("dve", mixT[olo:olo + 64, 5 + h // 2, :], osb[0][olo:olo + 64, :], bk[olo:olo + 64, :], ALU.mult,
                               [osb[1], Rb], [R_mix[5 + h // 2]])
                    if t == 0 and l == 0:
                        chk('yc', mixT[:, 5, :], [R_mix[5]])

                    k.barrier([R_ar])
                    Cm = carve([("y", 4096), ("rs", 512), ("s0", 512), ("s1", 512), ("u0", 512), ("u1", 512)])
                    yT = Cm["y"][0].rearrange("p (c t) -> p c t", t=NT)
                    R_y = Cm["y"][1]

                    def resid(gbase):
                        rms_rstd(Cm["rs"], [yT[:, c, :] for c in range(8)], [R_y], D, 1e-6, [Cm["s0"][0], Cm["s1"][0]], [Cm["s0"][1], Cm["s1"][1]])
                        for c in range(8):
                            u, Ru = Cm["u%d" % (c % 2)]
                            stt("dve", u, yT[:, c, :], dc(l, gbase, c), Cm["rs"][0], ALU.mult, ALU.mult, [R_y, Cm["rs"][1], R_dcol], [Ru])
                            tt("dve", xT[:, c, :], xT[:, c, :], u, ALU.add, [R_x[c], Ru], [R_x[c]])

                    wv_, Rw = wload(wout_d[l].rearrange("(c p) n -> p c n", p=128), [128, 8, 1024])
                    for oc in range(8):
                        bk, Rb = bank((0, 1))
                        for kc in range(8):
                            mm(bk[:], wv_[:, kc, oc * 128:(oc + 1) * 128], mixT[:, kc, :], kc == 0, kc == 7, [Rw, R_mix[kc]], [Rb], inc=(kc == 7))
                        cp("act", yT[:, oc, :], bk[:], [Rb], [R_y])
                    resid(GG1)
                    if t == 0 and l == 0:
                        chk('x1', xT[:, 0, :], [R_x[0]])
                    rms_rstd(Cm["rs"], [xT[:, c, :] for c in range(8)], R_x, D, 1e-6, [Cm["s0"][0], Cm["s1"][0]], [Cm["s0"][1], Cm["s1"][1]])
                    for c in range(8):
                        u, Ru = Cm["u%d" % (c % 2)]
                        stt("dve", u, xT[:, c, :], dc(l, GM2, c), Cm["rs"][0], ALU.mult, ALU.mult, [R_x[c], Cm["rs"][1], R_dcol], [Ru])
                        act(hT[:, c, :], u, AF.Identity, [Ru, R_dcol], [R_h], bias=dc(l, MOD + 24, c))
                    for pc in range(4):
                        wv_, Rw = wload(ff1_d[l, :, pc * 1024:(pc + 1) * 1024].rearrange("(c p) n -> p c n", p=128), [128, 8, 1024])
                        for j in range(8):
                            hc = pc * 8 + j
                            bk, Rb = bank((0, 1))
                            for kc in range(8):
                                mm(bk[:], wv_[:, kc, j * 128:(j + 1) * 128], hT[:, kc, :], kc == 0, kc == 7, [Rw, R_h], [Rb], inc=(kc == 7))
                            s_, Rs_ = Cm["s%d" % (hc % 2)]
                            act(s_, bk[:], AF.Square, [Rb], [Rs_])
                            stt("dve", hid[:, hc, :], bk[:], 0.0, s_, ALU.is_gt, ALU.mult, [Rb, Rs_], [R_big])
                    for pc in range(4):
                        wv_, Rw = wload(ff2_d[l, :, pc * 256:(pc + 1) * 256].rearrange("(c p) n -> p c n", p=128), [128, 32, 256])
                        for j in range(2):
                            oc = pc * 2 + j
                            bk, Rb = bank((0, 1))
                            for kc in range(32):
                                mm(bk[:], wv_[:, kc, j * 128:(j + 1) * 128], hid[:, kc, :], kc == 0, kc == 31, [Rw, R_big], [Rb], inc=(kc == 31))
                            cp("act", yT[:, oc, :], bk[:], [Rb], [R_y])
                    resid(GG2)
                    if t == 0 and l == 0:
                        chk('x2', xT[:, 0, :], [R_x[0]])
                for c in range(8):
                    k.dma("sp", oT_d[c * 128:(c + 1) * 128, tok], xT[:, c, :], [R_x[c]], [R_out], accumulate=True)
        except _Stop:
            pass
        k.finish([R_out, R_dbg])
        print("instructions emitted:", k.nins, "sems:", k.nsem)
    _DBG_SLOTS.clear()
    _DBG_SLOTS.update(dbg_slots)
    return nc


def _consts():
    c = np.zeros((128, NCONST), np.float32)
    c[:, K_ID:K_ID + 128] = np.eye(128)
    c[:, K_ONES:K_ONES + 128] = 1.0
    i = np.arange(128)
    same = (i[:, None] // 64) == (i[None, :] // 64)
    c[:, K_BLK:K_BLK + 128] = same
    li = i % 64
    c[:, K_SU:K_SU + 128] = same & (li[:, None] < li[None, :])
    c[:, K_SL:K_SL + 128] = same & (li[:, None] > li[None, :])
    c[:, K_UIS:K_UIS + 64] = li[:, None] <= np.arange(64)[None, :]
    c[:, K_CM:K_CM + 128] = i[:, None] <= i[None, :]
    rm = np.ones(512, np.float32)
    rm[::64] = 0.0
    c[:, K_RM:K_RM + 512] = rm[None, :]
    sw = np.zeros((128, 128), np.float32)
    sw[(i + 64) % 128, i] = 1.0
    c[:, K_SWAP:K_SWAP + 128] = sw
    rot = np.zeros((128, 128), np.float32)
    for r in range(16):
        rot[64 + r + 16, 64 + r] = -1.0
        rot[64 + r, 64 + r + 16] = 1.0
    c[:, K_ROT:K_ROT + 128] = rot
    inv = (1.0 / (np.float32(10000.0) ** (np.arange(0, 32, 2, dtype=np.float32) / np.float32(32)))).astype(np.float32)
    c[64:96, K_IF] = np.concatenate([inv, inv])
    return c


def _colize(v):
    v = np.asarray(v, np.float32).reshape(-1, 128)
    return np.ascontiguousarray(v.T)


def _prep(inp, T):
    f = lambda a: np.ascontiguousarray(np.asarray(a, np.float32))
    cols = np.zeros((L, 128, NCOL), np.float32)
    wl = np.zeros((L, 128, 1152), np.float32)
    for l in range(L):
        def put(base, v):
            cc = _colize(v)
            cols[l, :, base:base + cc.shape[1]] = cc
        put(GPM, inp["g_pre_mix"][l]); put(GQM, inp["g_post_mix"][l]); put(GPF, inp["g_pre_ffn"][l]); put(GQF, inp["g_post_ffn"][l])
        put(BADA, inp["b_ada"][l])
        put(MU, inp["rwkv_mu"][l]); put(W0, inp["rwkv_w0"][l]); put(A0, inp["rwkv_a0"][l]); put(KK, inp["rwkv_k_k"][l])
        put(KA, inp["rwkv_k_a"][l]); put(RK, np.asarray(inp["rwkv_r_k"][l]).reshape(-1)); put(LNW, inp["rwkv_ln_w"][l])
        put(LNB, inp["rwkv_ln_b"][l]); put(CB, inp["conv_b"][l]); put(CLW, inp["conv_ln_w"][l]); put(CLB, inp["conv_ln_b"][l])
        put(QN, inp["mla_q_norm"][l]); put(KVN, inp["mla_kv_norm"][l])
        cw = np.asarray(inp["conv_w"][l], np.float32)
        for j in range(31):
            for ch in range(2):
                cols[l, :, CW + j * 2 + ch] = cw[j, ch * 128:(ch + 1) * 128]
        wl[l, 0:32, 0:384] = inp["rwkv_w2"][l]
        wl[l, 32:64, 384:768] = inp["rwkv_a2"][l]
        wl[l, 64:128, 768:1152] = inp["rwkv_g2"][l]
    w_in = np.asarray(inp["w_in"], np.float32)
    win = np.concatenate([w_in[:, :, 0:2304], w_in[:, :, 2240:2304], w_in[:, :, 2304:2336]], axis=2)
    wukv = np.asarray(inp["mla_w_ukv"], np.float32).reshape(L, 256, 6, 128)
    wk = np.ascontiguousarray(wukv[:, :, :, 0:64].reshape(L, 256, 384))
    wv = np.ascontiguousarray(wukv[:, :, :, 64:128].reshape(L, 256, 384))
    shared = {
        "wada": f(inp["w_ada"]), "win": f(win), "wout": f(inp["w_out"]), "ff1": f(inp["w_ff1"]), "ff2": f(inp["w_ff2"]),
        "wl": wl, "wuq": f(inp["mla_w_uq"]), "wk": wk, "wv": wv, "cols": cols, "consts": _consts(),
    }
    maps = []
    x = np.asarray(inp["x"], np.float32)
    c = np.asarray(inp["c"], np.float32)
    pos = np.asarray(inp["positions"], np.int32)
    nb = x.shape[0]
    for core in range(8):
        b = core % nb
        m = dict(shared)
        m["xT"] = np.ascontiguousarray(x[b, :T].T)
        m["cT"] = _colize(c[b])
        m["pos"] = np.ascontiguousarray(pos[b:b + 1, :T])
        maps.append(m)
    return maps


_NC_CACHE = {}
_DBG_SLOTS = {}


def run_dbg(inp, T, names, stop):
    nc = build(T, dbg=(names, stop))
    maps = _prep(inp, T)
    res = run_bass_kernel_spmd(nc, maps, core_ids=list(range(8)))
    return res.results[0]["dbg"], dict(_DBG_SLOTS)


def run(inp, T):
    if T not in _NC_CACHE:
        _NC_CACHE[T] = build(T)
    nc = _NC_CACHE[T]
    maps = _prep(inp, T)
    res = run_bass_kernel_spmd(nc, maps, core_ids=list(range(8)))
    nb = np.asarray(inp["x"]).shape[0]
    out = np.stack([np.ascontiguousarray(res.results[b]["oT"].T) for b in range(nb)], axis=0)
    return out.astype(np.float32)


def kernel(**inputs):
    T = np.asarray(inputs["x"]).shape[1]
    return run(inputs, T)
```
